# Optimizing a Trainium2 kernel written in Bass

```python
import math
import jax, jax.numpy as jnp
from jax import lax
import numpy as np

D_MODEL = 2048
BATCH = 2
SEQ = 16384
DEPTH = 2
DEC_BATCH = 4
DEC_SEQ = 2048
PAST_LEN = 128

MIX_W = D_MODEL
GROUP_W = MIX_W // 4
HEAD_DIM = 64
N_HEADS = GROUP_W // HEAD_DIM
N_KV = 2
N_REP = N_HEADS // N_KV
KV_W = N_KV * HEAD_DIM
SCONV_K = 3
CONF_K = 31
WINDOW = 128
WB = 128
QB = 128
GRID_W = 64
ROPE_THETA = 10000.0
D_FF = 4 * D_MODEL
RMS_EPS = 1e-6
LN_EPS = 1e-5
SPLIT_SIZES = (GROUP_W, GROUP_W, GROUP_W,
               GROUP_W, KV_W, KV_W,
               GROUP_W, KV_W, KV_W,
               GROUP_W, GROUP_W)
IN_W = sum(SPLIT_SIZES)

kernel_name = 'hymba_style_hybrid_encoder'


def _rms(x, g):
    xf = x.astype(jnp.float32)
    y = xf * lax.rsqrt(jnp.mean(xf * xf, axis=-1, keepdims=True) + RMS_EPS)
    return (y * g.astype(jnp.float32)).astype(x.dtype)


def _layernorm(x, g, b):
    xf = x.astype(jnp.float32)
    mu = jnp.mean(xf, axis=-1, keepdims=True)
    var = jnp.mean(jnp.square(xf - mu), axis=-1, keepdims=True)
    y = (xf - mu) * lax.rsqrt(var + LN_EPS)
    return (y * g.astype(jnp.float32) + b.astype(jnp.float32)).astype(x.dtype)


def _rope_cos_sin(pos, dim):
    inv = 1.0 / (ROPE_THETA ** (jnp.arange(0, dim, 2, dtype=jnp.float32) / dim))
    ang = pos.astype(jnp.float32)[:, None] * inv[None, :]
    return jnp.cos(ang), jnp.sin(ang)


def _apply_rope(x, cos, sin):
    xf = x.astype(jnp.float32)
    half = xf.shape[-1] // 2
    x1, x2 = xf[..., :half], xf[..., half:]
    c = cos[None, :, None, :]
    s = sin[None, :, None, :]
    return jnp.concatenate([x1 * c - x2 * s, x2 * c + x1 * s], axis=-1).astype(x.dtype)


def _dwconv(x, w, b=None):
    k, c = w.shape
    y = lax.conv_general_dilated(x, w.astype(x.dtype)[:, None, :], window_strides=(1,),
                                 padding=[(k // 2, k // 2)],
                                 dimension_numbers=('NWC', 'WIO', 'NWC'),
                                 feature_group_count=c)
    if b is not None:
        y = y + b.astype(x.dtype)
    return y


def _global_attn(q, k, v):
    b, s, _, d = q.shape
    nb = s // QB
    scale = 1.0 / math.sqrt(d)
    qb = (q * scale).reshape(b, nb, QB, N_KV, N_REP, d).transpose(1, 0, 2, 3, 4, 5)

    def block(qi):
        sc = jnp.einsum('bqkgd,bskd->bkgqs', qi, k).astype(jnp.float32)
        p = jax.nn.softmax(sc, axis=-1).astype(v.dtype)
        return jnp.einsum('bkgqs,bskd->bqkgd', p, v)

    o = lax.map(block, qb)
    return o.transpose(1, 0, 2, 3, 4, 5).reshape(b, s, N_HEADS * d)


def _window_attn(q, k, v, sink):
    b, s, _, d = q.shape
    nb = s // WB
    scale = 1.0 / math.sqrt(d)
    qb = (q * scale).reshape(b, nb, WB, N_KV, N_REP, d)

    def band(t):
        tp = jnp.pad(t, ((0, 0), (WB, WB), (0, 0), (0, 0))).reshape(b, nb + 2, WB, N_KV, d)
        return jnp.concatenate([tp[:, :-2], tp[:, 1:-1], tp[:, 2:]], axis=2)

    kw, vw = band(k), band(v)
    n = jnp.arange(nb)[:, None, None]
    i = jnp.arange(WB)[None, :, None]
    j = jnp.arange(3 * WB)[None, None, :]
    qpos = n * WB + i
    kpos = n * WB - WB + j
    valid = (jnp.abs(qpos - kpos) <= WINDOW) & (kpos >= 0) & (kpos < s)
    sc = jnp.einsum('bnqkgd,bnskd->bnkgqs', qb, kw).astype(jnp.float32)
    sc = jnp.where(valid[None, :, None, None, :, :], sc, -1e30)
    sk = sink.astype(jnp.float32).reshape(1, 1, N_KV, N_REP, 1, 1)
    m = jnp.maximum(jnp.max(sc, axis=-1, keepdims=True), sk)
    e = jnp.exp(sc - m)
    p = (e / (jnp.sum(e, axis=-1, keepdims=True) + jnp.exp(sk - m))).astype(v.dtype)
    o = jnp.einsum('bnkgqs,bnskd->bnqkgd', p, vw)
    return o.reshape(b, s, N_HEADS * d)


def _layer(x, pre_mix_g, w_in, conv_a_w, q_norm_g, k_norm_g, sink_c, conv_d_w, conv_d_b,
           ln_d_g, ln_d_b, w_out, post_mix_g, pre_ffn_g, w_ff1, w_ff2, post_ffn_g):
    b, s, _ = x.shape
    rows = s // GRID_W
    row_pos = jnp.repeat(jnp.arange(rows), GRID_W)
    col_pos = jnp.tile(jnp.arange(GRID_W), rows)
    lin_pos = jnp.arange(s)

    h = _rms(x, pre_mix_g)
    z = jnp.einsum('bsd,de->bse', h, w_in.astype(h.dtype))
    idx = np.cumsum(SPLIT_SIZES)[:-1].tolist()
    (a_b, a_c, a_v, qB, kB, vB, qC, kC, vC, d_a, d_g) = jnp.split(z, idx, axis=-1)

    y_a = a_b * _dwconv(a_c * a_v, conv_a_w)

    qB = _rms(qB.reshape(b, s, N_HEADS, HEAD_DIM), q_norm_g)
    kB = _rms(kB.reshape(b, s, N_KV, HEAD_DIM), k_norm_g)
    vB = vB.reshape(b, s, N_KV, HEAD_DIM)
    half = HEAD_DIM // 2
    rc, rs = _rope_cos_sin(row_pos, half)
    cc, cs = _rope_cos_sin(col_pos, half)

    def axial(t):
        return jnp.concatenate([_apply_rope(t[..., :half], rc, rs),
                                _apply_rope(t[..., half:], cc, cs)], axis=-1)

    y_b = _global_attn(axial(qB), axial(kB), vB)

    lc, ls = _rope_cos_sin(lin_pos, HEAD_DIM)
    qC = _apply_rope(qC.reshape(b, s, N_HEADS, HEAD_DIM), lc, ls)
    kC = _apply_rope(kC.reshape(b, s, N_KV, HEAD_DIM), lc, ls)
    vC = vC.reshape(b, s, N_KV, HEAD_DIM)
    y_c = _window_attn(qC, kC, vC, sink_c)

    u = d_a * jax.nn.sigmoid(d_g)
    u = _dwconv(u, conv_d_w, conv_d_b)
    y_d = jax.nn.silu(_layernorm(u, ln_d_g, ln_d_b))

    y = jnp.concatenate([y_a, y_b, y_c, y_d], axis=-1)
    y = jnp.einsum('bse,ed->bsd', y, w_out.astype(y.dtype))
    x = x + _rms(y, post_mix_g)

    h = _rms(x, pre_ffn_g)
    u = jnp.square(jax.nn.relu(jnp.einsum('bsd,df->bsf', h, w_ff1.astype(h.dtype))))
    y = jnp.einsum('bsf,fd->bsd', u, w_ff2.astype(u.dtype))
    return x + _rms(y, post_ffn_g)


def setup_inputs(seed: int = 0) -> dict:
    key = jax.random.key(seed)
    ks = jax.random.split(key, 24)
    f32 = jnp.float32

    def nrm(k, shape, scale):
        return jax.random.normal(k, shape, f32) * scale

    def gain(k, shape):
        return 1.0 + 0.05 * jax.random.normal(k, shape, f32)

    return {
        'x_prompt': nrm(ks[0], (BATCH, SEQ, D_MODEL), 1.0),
        'x_sample': nrm(ks[1], (DEC_BATCH, DEC_SEQ, D_MODEL), 1.0),
        'pre_mix_g': gain(ks[2], (DEPTH, D_MODEL)),
        'w_in': nrm(ks[3], (DEPTH, D_MODEL, IN_W), D_MODEL ** -0.5),
        'conv_a_w': nrm(ks[4], (DEPTH, SCONV_K, GROUP_W), SCONV_K ** -0.5),
        'q_norm_g': gain(ks[5], (DEPTH, HEAD_DIM)),
        'k_norm_g': gain(ks[6], (DEPTH, HEAD_DIM)),
        'sink_c': nrm(ks[7], (DEPTH, N_HEADS), 0.5),
        'conv_d_w': nrm(ks[8], (DEPTH, CONF_K, GROUP_W), CONF_K ** -0.5),
        'conv_d_b': nrm(ks[9], (DEPTH, GROUP_W), 0.02),
        'ln_d_g': gain(ks[10], (DEPTH, GROUP_W)),
        'ln_d_b': nrm(ks[11], (DEPTH, GROUP_W), 0.02),
        'w_out': nrm(ks[12], (DEPTH, MIX_W, D_MODEL), MIX_W ** -0.5),
        'post_mix_g': gain(ks[13], (DEPTH, D_MODEL)),
        'pre_ffn_g': gain(ks[14], (DEPTH, D_MODEL)),
        'w_ff1': nrm(ks[15], (DEPTH, D_MODEL, D_FF), D_MODEL ** -0.5),
        'w_ff2': nrm(ks[16], (DEPTH, D_FF, D_MODEL), D_FF ** -0.5),
        'post_ffn_g': gain(ks[17], (DEPTH, D_MODEL)),
    }


def reference(x_prompt, x_sample, pre_mix_g, w_in, conv_a_w, q_norm_g, k_norm_g, sink_c,
              conv_d_w, conv_d_b, ln_d_g, ln_d_b, w_out, post_mix_g, pre_ffn_g, w_ff1,
              w_ff2, post_ffn_g):
    yp = x_prompt
    ys = x_sample
    for l in range(DEPTH):
        params = (pre_mix_g[l], w_in[l], conv_a_w[l], q_norm_g[l], k_norm_g[l], sink_c[l],
                  conv_d_w[l], conv_d_b[l], ln_d_g[l], ln_d_b[l], w_out[l], post_mix_g[l],
                  pre_ffn_g[l], w_ff1[l], w_ff2[l], post_ffn_g[l])
        yp = _layer(yp, *params)
        ys = _layer(ys, *params)
    return (yp, ys)
```

```python
import bisect
import contextlib
import numpy as np
import ml_dtypes
import concourse.bass as bass
import concourse.mybir as mybir
from concourse.bass_utils import run_bass_kernel_spmd

F32 = mybir.dt.float32
BF16 = mybir.dt.bfloat16
ALU = mybir.AluOpType
AF = mybir.ActivationFunctionType
AX = mybir.AxisListType

D = 2048
INW = 4096
DFF = 8192
NL = 2
TBK = 512
ENGS = ["pe", "act", "dve", "pool", "sp"]
N_DMA_SEMS = 16


class TV:
    __slots__ = ("ap", "keys")

    def __init__(self, ap, keys):
        self.ap = ap
        self.keys = tuple(keys) if isinstance(keys, (list, tuple)) else (keys,)


class Op:
    __slots__ = ("eng", "fn", "reads", "writes", "kind", "idx", "slot", "slot_n",
                 "waits", "target", "clock")

    def __init__(self, eng, fn, reads, writes, kind):
        self.eng = eng
        self.fn = fn
        self.reads = reads
        self.writes = writes
        self.kind = kind
        self.waits = []
        self.target = False
        self.idx = None
        self.slot = None
        self.slot_n = None
        self.clock = None


def _keys(lst):
    out = []
    for r in lst:
        if isinstance(r, TV):
            out.extend(r.keys)
        elif isinstance(r, list):
            out.extend(_keys(r))
        else:
            out.append(r)
    return tuple(out)


class Prog:
    def __init__(self, nc):
        self.nc = nc
        self.ops = []

    def op(self, eng, fn, reads=(), writes=(), kind="c"):
        o = Op(eng, fn, _keys(reads), _keys(writes), kind)
        self.ops.append(o)
        return o

    def barrier(self):
        for e in ENGS:
            self.ops.append(Op(e, None, (), (), "b"))

    def analyze(self):
        last_w = {}
        readers = {}
        eng_count = {e: 0 for e in ENGS}
        eng_last = {e: None for e in ENGS}
        seen = {e: {} for e in ENGS}
        slot_last = [None] * N_DMA_SEMS
        slot_cnt = [0] * N_DMA_SEMS
        rr = 0
        for o in self.ops:
            deps = []
            if o.kind == "b":
                for e in ENGS:
                    if eng_last[e] is not None:
                        deps.append(eng_last[e])
                for s in slot_last:
                    if s is not None:
                        deps.append(s)
            else:
                for k in o.reads:
                    w = last_w.get(k)
                    if w is not None:
                        deps.append(w)
                for k in o.writes:
                    w = last_w.get(k)
                    if w is not None:
                        deps.append(w)
                    rs = readers.get(k)
                    if rs:
                        deps.extend(rs)
            if o.kind == "d":
                s = rr % N_DMA_SEMS
                rr += 1
                o.slot = s
                if slot_last[s] is not None:
                    deps.append(slot_last[s])
                slot_cnt[s] += 1
                o.slot_n = slot_cnt[s]
                slot_last[s] = o
            E = o.eng
            sE = seen[E]
            for p in deps:
                if p is o:
                    continue
                if p.kind == "d":
                    key = ("s", p.slot)
                    val = p.slot_n
                elif p.kind == "c":
                    if p.eng == "pe" and E == "pe" and o.kind != "b":
                        continue
                    if p.eng == E and o.kind == "b":
                        continue
                    key = p.eng
                    val = p.idx
                else:
                    continue
                if sE.get(key, 0) >= val:
                    continue
                o.waits.append((key, val))
                p.target = True
                for kk, vv in p.clock.items():
                    if sE.get(kk, 0) < vv:
                        sE[kk] = vv
                sE[key] = val
            if o.kind == "c":
                eng_count[E] += 1
                o.idx = eng_count[E]
                eng_last[E] = o
            c = dict(sE)
            if o.kind == "c":
                c[E] = o.idx
            elif o.kind == "d":
                c[("s", o.slot)] = o.slot_n
            o.clock = c
            if o.kind in ("c", "d"):
                for k in o.reads:
                    readers.setdefault(k, []).append(o)
                for k in o.writes:
                    last_w[k] = o
                    readers[k] = []
        self.tidx = {e: [] for e in ENGS}
        for o in self.ops:
            if o.kind == "c" and o.target:
                self.tidx[o.eng].append(o.idx)
        for o in self.ops:
            o.clock = None

    def emit(self):
        nc = self.nc
        self.analyze()
        with contextlib.ExitStack() as es:
            esem = {e: es.enter_context(nc.semaphore("sem_" + e)) for e in ENGS}
            dsem = [es.enter_context(nc.semaphore("dsem%d" % i)) for i in range(N_DMA_SEMS)]
            block = es.enter_context(nc.Block())
            per = {e: [o for o in self.ops if o.eng == e] for e in ENGS}
            tidx = self.tidx

            def run(eng_name, e):
                for o in per[eng_name]:
                    ws = []
                    for key, val in o.waits:
                        if isinstance(key, tuple):
                            ws.append((dsem[key[1]], 16 * val))
                        else:
                            ws.append((esem[key], bisect.bisect_right(tidx[key], val)))
                    if o.kind in ("w", "b"):
                        for sm_, v_ in ws:
                            e.wait_ge(sm_, v_)
                        continue
                    for sm_, v_ in ws[:-1]:
                        e.wait_ge(sm_, v_)
                    ins = o.fn(e)
                    if ws:
                        ins._wait_ge(ws[-1][0], ws[-1][1])
                    if o.kind == "c":
                        if o.target:
                            ins.then_inc(esem[eng_name], 1)
                    else:
                        ins.then_inc(dsem[o.slot], 16)

            @block.tensor
            def _(e):
                run("pe", e)

            @block.scalar
            def _(e):
                run("act", e)

            @block.vector
            def _(e):
                run("dve", e)

            @block.gpsimd
            def _(e):
                run("pool", e)

            @block.sync
            def _(e):
                run("sp", e)


def win_chunks():
    ch = []
    for j in range(4):
        ch.append(("ab", j, [(128 * j, 128, 0)]))
    for j in range(4):
        ch.append(("ac", j, [(512 + 128 * j, 128, 0)]))
        ch.append(("av", j, [(1024 + 128 * j, 128, 0)]))
    for j in range(4):
        ch.append(("dg", j, [(3584 + 128 * j, 128, 0)]))
        ch.append(("da", j, [(3072 + 128 * j, 128, 0)]))
    for j in range(4):
        ch.append(("qb", j, [(1536 + 64 * j, 64, 0), (1536 + 64 * (4 + j), 64, 64)]))
    ch.append(("kb", 0, [(2048, 128, 0)]))
    ch.append(("vb", 0, [(2176, 128, 0)]))
    for j in range(4):
        ch.append(("qc", j, [(2304 + 64 * j, 64, 0), (2304 + 64 * (4 + j), 64, 64)]))
    ch.append(("kc", 0, [(2816, 128, 0)]))
    ch.append(("vc", 0, [(2944, 128, 0)]))
    return ch


def rope_tables(S):
    theta = np.float32(10000.0)
    t = np.arange(S)
    inv16 = (np.float32(1.0) / (theta ** (np.arange(0, 32, 2, dtype=np.float32) / np.float32(32)))).astype(np.float32)
    inv32 = (np.float32(1.0) / (theta ** (np.arange(0, 64, 2, dtype=np.float32) / np.float32(64)))).astype(np.float32)
    rowp = (t // 64).astype(np.float32)
    colp = (t % 64).astype(np.float32)
    linp = t.astype(np.float32)
    cb = np.zeros((128, S), np.float32)
    sb_ = np.zeros((128, S), np.float32)
    cc = np.zeros((128, S), np.float32)
    sc = np.zeros((128, S), np.float32)
    for p in range(128):
        d = p % 64
        pos = rowp if d < 32 else colp
        ang = (pos * inv16[d % 16]).astype(np.float32)
        cb[p] = np.cos(ang)
        sb_[p] = np.sin(ang)
        ang2 = (linp * inv32[d % 32]).astype(np.float32)
        cc[p] = np.cos(ang2)
        sc[p] = np.sin(ang2)
    return cb, sb_, cc, sc


def const_mats():
    ident = np.eye(128, dtype=np.float32)
    rotb = np.zeros((128, 128), np.float32)
    rotc = np.zeros((128, 128), np.float32)
    for m in range(128):
        if m % 32 < 16:
            rotb[m + 16, m] = -1.0
        else:
            rotb[m - 16, m] = 1.0
        if m % 64 < 32:
            rotc[m + 32, m] = -1.0
        else:
            rotc[m - 32, m] = 1.0
    swap = np.zeros((128, 128), np.float32)
    blk = np.zeros((128, 128), np.float32)
    for i in range(64):
        swap[i + 64, i] = 1.0
        swap[i, i + 64] = 1.0
    blk[:64, :64] = 1.0 / 64
    blk[64:, 64:] = 1.0 / 64
    mask = np.zeros((128, 3, 384), np.float32)
    i = np.arange(128)[:, None]
    jj = np.arange(384)[None, :]
    band = np.abs(i - (jj - 128)) <= 128
    mask[:, 1, :] = np.where(band, 0.0, -1e30)
    mask[:, 0, :] = np.where(band & (jj >= 128), 0.0, -1e30)
    mask[:, 2, :] = np.where(band & (jj < 256), 0.0, -1e30)
    cm = np.concatenate([ident, rotb, rotc, swap, blk], axis=1)
    return cm, mask.reshape(128, 3 * 384)


PP_GPRE, PP_GPOST, PP_GFFN, PP_GPFFN = 0, 32, 64, 96
PP_GQ, PP_GK, PP_SINK, PP_CAW, PP_CDW, PP_CDB, PP_LNG, PP_LNB = 128, 130, 132, 148, 172, 420, 428, 436
PP_N = 444


def pack_params(inp):
    pp = np.zeros((128, PP_N), np.float32)
    for l in range(NL):
        pp[:, PP_GPRE + 16 * l:PP_GPRE + 16 * l + 16] = inp["pre_mix_g"][l].reshape(16, 128).T
        pp[:, PP_GPOST + 16 * l:PP_GPOST + 16 * l + 16] = inp["post_mix_g"][l].reshape(16, 128).T
        pp[:, PP_GFFN + 16 * l:PP_GFFN + 16 * l + 16] = inp["pre_ffn_g"][l].reshape(16, 128).T
        pp[:, PP_GPFFN + 16 * l:PP_GPFFN + 16 * l + 16] = inp["post_ffn_g"][l].reshape(16, 128).T
        pp[:, PP_GQ + l] = np.tile(inp["q_norm_g"][l], 2)
        pp[:, PP_GK + l] = np.tile(inp["k_norm_g"][l], 2)
        pp[:, PP_SINK + 8 * l:PP_SINK + 8 * l + 8] = inp["sink_c"][l][None, :]
        for j in range(4):
            pp[:, PP_CAW + 12 * l + 3 * j:PP_CAW + 12 * l + 3 * j + 3] = inp["conv_a_w"][l][:, 128 * j:128 * j + 128].T
            pp[:, PP_CDW + 124 * l + 31 * j:PP_CDW + 124 * l + 31 * j + 31] = inp["conv_d_w"][l][:, 128 * j:128 * j + 128].T
            pp[:, PP_CDB + 4 * l + j] = inp["conv_d_b"][l][128 * j:128 * j + 128]
            pp[:, PP_LNG + 4 * l + j] = inp["ln_d_g"][l][128 * j:128 * j + 128]
            pp[:, PP_LNB + 4 * l + j] = inp["ln_d_b"][l][128 * j:128 * j + 128]
    return pp


def build_program(SEGS):
    nc = bass.Bass("TRN2", target_bir_lowering=False)
    P = Prog(nc)
    WCH = win_chunks()

    def din(name, shape, dt=F32):
        return nc.dram_tensor(name, list(shape), dt, kind="ExternalInput")

    w_in = din("w_in", [NL, D, INW])
    w_out = din("w_out", [NL, D, D])
    w_ff1 = din("w_ff1", [NL, D, DFF])
    w_ff2 = din("w_ff2", [NL, DFF, D])
    cm_d = din("cm", [128, 640])
    mask_d = din("mask", [128, 3 * 384])
    pp_d = din("pp", [128, PP_N])
    xin, yout, tabs = [], [], []
    for si, S in enumerate(SEGS):
        xin.append(din("x%d" % si, [S, D]))
        yout.append(nc.dram_tensor("y%d" % si, [S, D], F32, kind="ExternalOutput"))
        tabs.append([din("tab%d_%d" % (si, k), [128, S]) for k in range(4)])
    SM = max(SEGS)
    WIN = nc.dram_tensor("WIN", [NL, 32, 128, 2048], BF16)
    WOUT = nc.dram_tensor("WOUT", [NL, 16, 128, 2048], BF16)
    W1 = nc.dram_tensor("W1", [NL, 64, 128, 2048], BF16)
    W2 = nc.dram_tensor("W2", [NL, 16, 128, 8192], BF16)
    XT = nc.dram_tensor("XT", [16, 128, SM], F32)
    X1T = nc.dram_tensor("X1T", [16, 128, SM], F32)
    QB = nc.dram_tensor("QB", [4, 128, SM], BF16)
    QC = nc.dram_tensor("QC", [4, 128, SM], BF16)
    ABd = nc.dram_tensor("ABd", [4, 128, SM], BF16)
    ACV = nc.dram_tensor("ACV", [4, 128, SM + 4], BF16)
    Ud = nc.dram_tensor("Ud", [4, 128, SM + 30], BF16)
    KBT = nc.dram_tensor("KBT", [128, SM], BF16)
    VBd = nc.dram_tensor("VBd", [SM, 128], BF16)
    KCT = nc.dram_tensor("KCT", [128, SM + 256], BF16)
    VCd = nc.dram_tensor("VCd", [SM + 256, 128], BF16)
    VAUG = nc.dram_tensor("VAUG", [2, SM // 128, 128, 128], BF16)

    with contextlib.ExitStack() as es:
        def sb(name, shape, dt):
            return es.enter_context(nc.sbuf_tensor(name, list(shape), dt))

        def ps(name, shape, dt):
            return es.enter_context(nc.psum_tensor(name, list(shape), dt))

        class Ring:
            def __init__(self, name, n, shape, dt, maker=sb):
                self.t = [maker("%s%d" % (name, i), shape, dt) for i in range(n)]
                self.name = name
                self.i = 0

            def next(self):
                k = self.i % len(self.t)
                self.i += 1
                return TV(self.t[k], [(self.name, k)])

        big32 = sb("big32", [128, 16, 512], F32)
        h16 = sb("h16", [128, 16, 512], BF16)
        R80 = sb("R80", [128, 32768], BF16)
        wA = Ring("wA", 6, [128, 16, 128], BF16)
        cm_s = sb("cm_s", [128, 640], F32)
        cmb_s = sb("cmb_s", [128, 640], BF16)
        mask_s = sb("mask_s", [128, 3, 384], F32)
        pp_s = sb("pp_s", [128, PP_N], F32)
        onesb = sb("onesb", [128, 128], BF16)
        ones512 = sb("ones512", [128, 128], F32)
        epst = sb("epst", [128, 4], F32)
        rstd = sb("rstd", [128, 512], F32)
        sdt = sb("sdt", [128, 512], F32)
        sqr = Ring("sqr", 2, [128, 512], BF16)
        f32r = Ring("f32r", 4, [128, 512], F32)
        b16r = Ring("b16r", 4, [128, 512], BF16)
        xg = Ring("xg", 2, [128, 2, 512], F32)
        qpad = Ring("qpad", 2, [128, 2, 512], BF16)
        kcw = sb("kcw", [128, 2, 768], BF16)
        vcw = sb("vcw", [128, 6, 128], BF16)
        qcblk = R80[:, 22656:24704].rearrange("p (a b) -> p a b", a=4)
        vring = Ring("vring", 3, [128, 8, 128], BF16)
        vaA = Ring("vaA", 2, [128, 4, 128], BF16)
        vaB = Ring("vaB", 2, [128, 4, 128], BF16)
        smr = Ring("smr", 2, [128, 384], F32)
        er = Ring("er", 2, [128, 384], BF16)
        etr = Ring("etr", 2, [128, 3, 128], BF16)
        smallr = Ring("smallr", 8, [128, 4], F32)
        Rt = Ring("Rt", 2, [128, 8], F32)
        osb = Ring("osb", 2, [128, 512], BF16)
        abt = Ring("abt", 2, [128, 512], BF16)
        acvt = Ring("acvt", 2, [128, 514], BF16)
        class UtRing:
            def next(self):
                return TV(R80[:, 20480:22648].rearrange("p (a b) -> p a b", a=4), [("ut", 0)])
        ut = UtRing()
        cacc = R80[:, 16384:20480].bitcast(F32).rearrange("p (a b) -> p a b", a=4)
        vt16 = Ring("vt16", 2, [128, 4, 128], BF16)

        class TabRing:
            def __init__(self):
                self.i = 0

            def next(self):
                k = self.i % 4
                self.i += 1
                return TV(cacc[:, k, :], [("cacc", k)])
        tabr = TabRing()

        pS = Ring("pS", 3, [128, 512], F32, maker=ps)
        pO = [ps("pO%d" % i, [128, 512], F32) for i in range(2)]
        pX = ps("pX", [128, 512], F32)
        pT = Ring("pT", 2, [128, 1024], BF16, maker=ps)

        ident_f = cm_s[:, 0:128]
        rotb_f = cm_s[:, 128:256]
        rotc_f = cm_s[:, 256:384]
        swap_f = cm_s[:, 384:512]
        ident_b = cmb_s[:, 0:128]
        blk_b = cmb_s[:, 512:640]
        CM = "cm_s"
        CMB = "cmb_s"
        PPK = "pp_s"

        cnt = {"ld": 0, "ev": 0}

        def dma(out_ap, out_k, in_ap, in_k, eng="sp"):
            P.op(eng, lambda e: e.dma_start(out=out_ap, in_=in_ap), reads=[in_k] if not isinstance(in_k, list) else in_k,
                 writes=[out_k] if not isinstance(out_k, list) else out_k, kind="d")

        def mm(out_ap, out_k, lhsT, lk, rhs, rk, start, stop):
            P.op("pe", lambda e: e.matmul(out_ap, lhsT, rhs, start=start, stop=stop), reads=[lk, rk], writes=[out_k])

        def tr(out_ap, out_k, in_ap, in_k, idt, idk):
            P.op("pe", lambda e: e.transpose(out_ap, in_ap, idt), reads=[in_k, idk], writes=[out_k])

        def act(out_ap, out_k, in_ap, in_k, func, bias=None, scale=None, accum=None, extra_r=(), extra_w=()):
            def f(e):
                kw = {}
                if bias is not None:
                    kw["bias"] = bias
                if scale is not None:
                    kw["scale"] = scale
                if accum is not None:
                    kw["accum_out"] = accum
                return e.activation(out=out_ap, in_=in_ap, func=func, **kw)
            P.op("act", f, reads=[in_k] + list(extra_r), writes=[out_k] + list(extra_w))

        def tt(eng, out_ap, out_k, a, ak, b, bk, op):
            P.op(eng, lambda e: e.tensor_tensor(out=out_ap, in0=a, in1=b, op=op), reads=[ak, bk], writes=[out_k])

        def ts(eng, out_ap, out_k, a, ak, s1, s2, op0, op1=None, extra_r=()):
            eng = "dve"

            def f(e):
                if op1 is None:
                    return e.tensor_scalar(out=out_ap, in0=a, scalar1=s1, scalar2=None, op0=op0)
                return e.tensor_scalar(out=out_ap, in0=a, scalar1=s1, scalar2=s2, op0=op0, op1=op1)
            P.op(eng, f, reads=[ak] + list(extra_r), writes=[out_k])

        def stt(eng, out_ap, out_k, a, ak, s, b, bk, op0, op1, extra_r=()):
            eng = "dve"
            P.op(eng, lambda e: e.scalar_tensor_tensor(out=out_ap, in0=a, scalar=s, in1=b, op0=op0, op1=op1),
                 reads=[ak, bk] + list(extra_r), writes=[out_k])

        def cp(eng, out_ap, out_k, in_ap, in_k):
            if eng == "act":
                P.op("act", lambda e: e.copy(out=out_ap, in_=in_ap), reads=[in_k], writes=[out_k])
            else:
                P.op(eng, lambda e: e.tensor_copy(out=out_ap, in_=in_ap), reads=[in_k], writes=[out_k])

        def recip(out_ap, out_k, in_ap, in_k):
            P.op("dve", lambda e: e.reciprocal(out=out_ap, in_=in_ap), reads=[in_k], writes=[out_k])

        def memset(eng, ap, k, v):
            P.op(eng, lambda e: e.memset(ap, v), writes=[k])

        def evac_eng():
            cnt["ev"] += 1
            return "act" if cnt["ev"] % 2 else "dve"

        dma(cm_s[:], CM, cm_d.ap(), "cm_d")
        dma(mask_s[:], "mask_s", mask_d.ap().rearrange("p (a b) -> p a b", a=3), "mask_d")
        dma(pp_s[:], PPK, pp_d.ap(), "pp_d")
        cp("dve", cmb_s[:], CMB, cm_s[:], CM)
        memset("dve", onesb[:], "onesb", 1.0 / 2048)
        memset("dve", ones512[:], "ones512", 1.0 / 512)
        memset("dve", epst[:, 0:1], "epst", 1e-6)
        memset("dve", epst[:, 1:2], "epst", 64e-6)
        memset("dve", epst[:, 2:3], "epst", 1e-5)
        memset("dve", epst[:, 3:4], "epst", 0.0)
        EPS6 = epst[:, 0:1]
        EPS6x64 = epst[:, 1:2]
        EPS5 = epst[:, 2:3]

        st32 = [R80[:, 4096 * i:4096 * (i + 1)].bitcast(F32).rearrange("p (a b) -> p a b", a=16) for i in range(4)]
        st16 = [R80[:, 16384 + 2048 * i:16384 + 2048 * (i + 1)].rearrange("p (a b) -> p a b", a=16) for i in range(4)]
        pc = [0]

        def prep_tile(src_list, dst_ap, dst_k):
            i = pc[0] % 4
            pc[0] += 1
            s32, s16 = st32[i], st16[i]
            for n_, (sap, dap) in enumerate(src_list):
                dma(dap(s32), ("st32", i, n_), sap, "wsrc")
            eng = ["dve", "pool", "act"][pc[0] % 3]
            cp(eng, s16, ("st16", i), s32, [("st32", i, n_) for n_ in range(10)])
            dma(dst_ap, dst_k, s16, ("st16", i), eng="pool")

        for l in range(NL):
            wv = w_in.ap()[l].rearrange("(kc p) n -> p kc n", p=128)
            for oc, (typ, j, segs) in enumerate(WCH):
                srcs = []
                for (c0, n, off) in segs:
                    srcs.append((wv[:, :, c0:c0 + n], (lambda t, off=off, n=n: t[:, :, off:off + n])))
                prep_tile(srcs, WIN.ap()[l, oc].rearrange("p (a b) -> p a b", a=16), ("WIN", l, oc))
            wv = w_out.ap()[l].rearrange("(kc p) n -> p kc n", p=128)
            wrow = w_out.ap()[l]
            for oc in range(16):
                srcs = [(wv[:, 0:4, oc * 128:(oc + 1) * 128], (lambda t: t[:, 0:4, :])),
                        (wv[:, 8:16, oc * 128:(oc + 1) * 128], (lambda t: t[:, 8:16, :]))]
                for j in range(4):
                    for hf in range(2):
                        r0 = 512 + 64 * (j + 4 * hf)
                        srcs.append((wrow[r0:r0 + 64, oc * 128:(oc + 1) * 128],
                                     (lambda t, j=j, hf=hf: t[64 * hf:64 * hf + 64, 4 + j, :])))
                prep_tile(srcs, WOUT.ap()[l, oc].rearrange("p (a b) -> p a b", a=16), ("WOUT", l, oc))
            wv = w_ff1.ap()[l].rearrange("(kc p) n -> p kc n", p=128)
            for oc in range(64):
                prep_tile([(wv[:, :, oc * 128:(oc + 1) * 128], (lambda t: t))],
                          W1.ap()[l, oc].rearrange("p (a b) -> p a b", a=16), ("W1", l, oc))
            wv = w_ff2.ap()[l].rearrange("(kc p) n -> p kc n", p=128)
            for oc in range(16):
                for q in range(4):
                    prep_tile([(wv[:, 16 * q:16 * q + 16, oc * 128:(oc + 1) * 128], (lambda t: t))],
                              W2.ap()[l, oc].rearrange("p (a b) -> p a b", a=64)[:, 16 * q:16 * q + 16, :], ("W2", l, oc))
        P.barrier()

        def rms_stats(src_fn, src_k):
            for kc in range(16):
                sq = sqr.next()
                act(sq.ap[:], sq.keys[0], src_fn(kc), src_k, AF.Square)
                mm(pX[:], "pX", onesb[:], "onesb", sq.ap[:], sq.keys[0], kc == 0, kc == 15)
            act(sdt[:], "sdt", pX[:], "pX", AF.Sqrt, bias=EPS6, extra_r=["epst"])
            recip(rstd[:], "rstd", sdt[:], "sdt")

        for si, S in enumerate(SEGS):
            NB = S // TBK
            NC_ = S // 128
            TAB = tabs[si]
            KT = R80[:, 0:S]
            VG = min(8, NC_)
            uT = R80[:, 0:32768].rearrange("p (a b) -> p a b", a=64)
            zt = b16r.next()
            memset("dve", zt.ap[:], zt.keys[0], 0.0)
            for j in range(4):
                dma(ACV.ap()[j][:, 0:2], ("ACV", -1), zt.ap[:, 0:2], zt.keys[0])
                dma(ACV.ap()[j][:, S + 2:S + 4], ("ACV", NB), zt.ap[:, 0:2], zt.keys[0])
                dma(Ud.ap()[j][:, 0:15], ("U", -1), zt.ap[:, 0:15], zt.keys[0])
                dma(Ud.ap()[j][:, S + 15:S + 30], ("U", NB), zt.ap[:, 0:15], zt.keys[0])
            dma(KCT.ap()[:, 0:128], ("KCT", -1), zt.ap[:, 0:128], zt.keys[0])
            dma(KCT.ap()[:, S + 128:S + 256], ("KCT", NB), zt.ap[:, 0:128], zt.keys[0])
            dma(VCd.ap()[0:128, :], ("VC", -1), zt.ap[:, 0:128], zt.keys[0])
            dma(VCd.ap()[S + 128:S + 256, :], ("VC", NB), zt.ap[:, 0:128], zt.keys[0])

            for l in range(NL):
                gpre = lambda kc: pp_s[:, PP_GPRE + 16 * l + kc:PP_GPRE + 16 * l + kc + 1]
                gpost = lambda kc: pp_s[:, PP_GPOST + 16 * l + kc:PP_GPOST + 16 * l + kc + 1]
                gffn = lambda kc: pp_s[:, PP_GFFN + 16 * l + kc:PP_GFFN + 16 * l + kc + 1]
                gpffn = lambda kc: pp_s[:, PP_GPFFN + 16 * l + kc:PP_GPFFN + 16 * l + kc + 1]
                gq = pp_s[:, PP_GQ + l:PP_GQ + l + 1]
                gk = pp_s[:, PP_GK + l:PP_GK + l + 1]
                sink = lambda h: pp_s[:, PP_SINK + 8 * l + h:PP_SINK + 8 * l + h + 1]
                caw = lambda j, k: pp_s[:, PP_CAW + 12 * l + 3 * j + k:PP_CAW + 12 * l + 3 * j + k + 1]
                cdw = lambda j, k: pp_s[:, PP_CDW + 124 * l + 31 * j + k:PP_CDW + 124 * l + 31 * j + k + 1]
                cdb = lambda j: pp_s[:, PP_CDB + 4 * l + j:PP_CDB + 4 * l + j + 1]
                lng = lambda j: pp_s[:, PP_LNG + 4 * l + j:PP_LNG + 4 * l + j + 1]
                lnb = lambda j: pp_s[:, PP_LNB + 4 * l + j:PP_LNB + 4 * l + j + 1]

                for b in range(NB):
                    t0 = b * TBK
                    if l == 0:
                        for s in range(4):
                          for hx in range(2):
                            xs_ = xg.next()
                            xv = xs_.ap.rearrange("p a b -> p (a b)")
                            dma(xv, xs_.keys[0], xin[si].ap()[t0 + s * 128:t0 + (s + 1) * 128, 1024 * hx:1024 * (hx + 1)], ("xin", si))
                            for g2 in range(2):
                                g = 2 * hx + g2
                                pt = pS.next()
                                for q in range(4):
                                    kq = 4 * g2 + q
                                    tr(pt.ap[:, q * 128:(q + 1) * 128], pt.keys[0], xv[:, kq * 128:(kq + 1) * 128], xs_.keys[0], ident_f, CM)
                                cp(evac_eng(), big32[:, 4 * g:4 * g + 4, s * 128:(s + 1) * 128], ("big32", g),
                                   pt.ap.rearrange("p (a b) -> p a b", a=4), pt.keys[0])
                        for g in range(4):
                            dma(XT.ap()[4 * g:4 * g + 4, :, t0:t0 + TBK].rearrange("k p t -> p k t"), ("XT", b), big32[:, 4 * g:4 * g + 4, :],
                                ("big32", g), eng="pool")
                    else:
                        for g in range(4):
                            dma(big32[:, 4 * g:4 * g + 4, :], ("big32", g),
                                XT.ap()[4 * g:4 * g + 4, :, t0:t0 + TBK].rearrange("k p t -> p k t"), ("XT", b))
                    rms_stats(lambda kc: big32[:, kc, :], [("big32", g) for g in range(4)])
                    for kc in range(16):
                        eng = "dve" if kc % 2 == 0 else "pool"
                        stt(eng, h16[:, kc, :], ("h16", kc), big32[:, kc, :], ("big32", kc // 4), gpre(kc), rstd[:], "rstd",
                            ALU.mult, ALU.mult, extra_r=[PPK])
                    tb = []
                    for k in range(4):
                        tv = tabr.next()
                        dma(tv.ap[:], tv.keys[0], TAB[k].ap()[:, t0:t0 + TBK], ("tab", si))
                        tb.append(tv)
                    keep = {}
                    for oc, (typ, j, segs) in enumerate(WCH):
                        wt = wA.next()
                        dma(wt.ap[:], wt.keys[0], WIN.ap()[l, oc].rearrange("p (a b) -> p a b", a=16), ("WIN", l, oc))
                        z = pS.next()
                        for kc in range(16):
                            mm(z.ap[:], z.keys[0], wt.ap[:, kc, :], wt.keys[0], h16[:, kc, :], ("h16", kc), kc == 0, kc == 15)
                        zk = z.keys[0]
                        if typ == "ab":
                            o = b16r.next()
                            cp("act", o.ap[:], o.keys[0], z.ap[:], zk)
                            dma(ABd.ap()[j][:, t0:t0 + TBK], ("AB", b), o.ap[:], o.keys[0], eng="pool")
                        elif typ in ("ac", "dg"):
                            o = f32r.next()
                            if typ == "ac":
                                cp("act", o.ap[:], o.keys[0], z.ap[:], zk)
                            else:
                                act(o.ap[:], o.keys[0], z.ap[:], zk, AF.Sigmoid)
                            keep[typ] = o
                        elif typ in ("av", "da"):
                            c = keep["ac" if typ == "av" else "dg"]
                            o = b16r.next()
                            tt("dve", o.ap[:], o.keys[0], z.ap[:], zk, c.ap[:], c.keys[0], ALU.mult)
                            if typ == "av":
                                dma(ACV.ap()[j][:, 2 + t0:2 + t0 + TBK], ("ACV", b), o.ap[:], o.keys[0], eng="pool")
                            else:
                                dma(Ud.ap()[j][:, 15 + t0:15 + t0 + TBK], ("U", b), o.ap[:], o.keys[0], eng="pool")
                        elif typ in ("qb", "kb", "qc", "kc"):
                            qn = f32r.next()
                            if typ in ("qb", "kb"):
                                sq = sqr.next()
                                act(sq.ap[:], sq.keys[0], z.ap[:], zk, AF.Square)
                                mm(pX[:], "pX", blk_b, CMB, sq.ap[:], sq.keys[0], True, True)
                                if typ == "qb":
                                    act(sdt[:], "sdt", pX[:], "pX", AF.Sqrt, bias=EPS6x64, scale=64.0, extra_r=["epst"])
                                else:
                                    act(sdt[:], "sdt", pX[:], "pX", AF.Sqrt, bias=EPS6, extra_r=["epst"])
                                rs = f32r.next()
                                recip(rs.ap[:], rs.keys[0], sdt[:], "sdt")
                                stt("dve", qn.ap[:], qn.keys[0], z.ap[:], zk, gq if typ == "qb" else gk, rs.ap[:], rs.keys[0],
                                    ALU.mult, ALU.mult, extra_r=[PPK])
                                rot, cs, sn = rotb_f, tb[0], tb[1]
                            else:
                                if typ == "qc":
                                    act(qn.ap[:], qn.keys[0], z.ap[:], zk, AF.Copy, scale=0.125)
                                else:
                                    cp("act", qn.ap[:], qn.keys[0], z.ap[:], zk)
                                rot, cs, sn = rotc_f, tb[2], tb[3]
                            mm(pX[:], "pX", rot, CM, qn.ap[:], qn.keys[0], True, True)
                            t1 = f32r.next()
                            tt("pool", t1.ap[:], t1.keys[0], qn.ap[:], qn.keys[0], cs.ap[:], cs.keys[0], ALU.mult)
                            t2 = f32r.next()
                            tt("dve", t2.ap[:], t2.keys[0], pX[:], "pX", sn.ap[:], sn.keys[0], ALU.mult)
                            o = b16r.next()
                            tt("dve", o.ap[:], o.keys[0], t1.ap[:], t1.keys[0], t2.ap[:], t2.keys[0], ALU.add)
                            if typ == "qb":
                                dma(QB.ap()[j][:, t0:t0 + TBK], ("QB", b), o.ap[:], o.keys[0], eng="pool")
                            elif typ == "kb":
                                dma(KBT.ap()[:, t0:t0 + TBK], ("KBT", b), o.ap[:], o.keys[0], eng="pool")
                            elif typ == "qc":
                                dma(QC.ap()[j][:, t0:t0 + TBK], ("QC", b), o.ap[:], o.keys[0], eng="pool")
                            else:
                                dma(KCT.ap()[:, 128 + t0:128 + t0 + TBK], ("KCT", b), o.ap[:], o.keys[0], eng="pool")
                        elif typ in ("vb", "vc"):
                            o = b16r.next()
                            cp("act", o.ap[:], o.keys[0], z.ap[:], zk)
                            pt = pT.next()
                            for s in range(4):
                                tr(pt.ap[:, s * 128:(s + 1) * 128], pt.keys[0], o.ap[:, s * 128:(s + 1) * 128], o.keys[0], ident_b, CMB)
                            vt = vt16.next()
                            cp("dve", vt.ap[:], vt.keys[0], pt.ap[:, 0:512].rearrange("p (a b) -> p a b", a=4), pt.keys[0])
                            if typ == "vb":
                                dma(VBd.ap()[t0:t0 + TBK, :].rearrange("(s p) c -> p s c", p=128), ("VB", b), vt.ap[:], vt.keys[0], eng="pool")
                            else:
                                dma(VCd.ap()[128 + t0:128 + t0 + TBK, :].rearrange("(s p) c -> p s c", p=128), ("VC", b), vt.ap[:], vt.keys[0], eng="pool")
                P.barrier()

                for b in range(NB):
                    dma(KT[:, b * TBK:(b + 1) * TBK], ("KT", b), KBT.ap()[:, b * TBK:(b + 1) * TBK], ("KBT", b))
                for qq in range(2):
                    a_ = vaA.next()
                    memset("pool", a_.ap[:], a_.keys[0], 1.0)
                    b_ = vaB.next()
                    memset("pool", b_.ap[:], b_.keys[0], 1.0)
                for b in range(NB):
                    src = VBd.ap()[b * TBK:(b + 1) * TBK, :].rearrange("(s p) c -> p s c", p=128)
                    vt = vt16.next()
                    dma(vt.ap[:], vt.keys[0], src, ("VB", b))
                    a_ = vaA.next()
                    cp("pool", a_.ap[:, :, 0:64], a_.keys[0], vt.ap[:, :, 0:64], vt.keys[0])
                    dma(VAUG.ap()[0, 4 * b:4 * b + 4].rearrange("c k v -> k c v"), ("VAUG", b), a_.ap[:], a_.keys[0], eng="pool")
                    b_ = vaB.next()
                    cp("pool", b_.ap[:, :, 64:128], b_.keys[0], vt.ap[:, :, 64:128], vt.keys[0])
                    dma(VAUG.ap()[1, 4 * b:4 * b + 4].rearrange("c k v -> k c v"), ("VAUG", b), b_.ap[:], b_.keys[0], eng="pool")
                for qq in range(2):
                    qp = qpad.next()
                    memset("pool", qp.ap[:], qp.keys[0], 0.0)
                memset("pool", kcw[:], "kcw", 0.0)

                for b in range(NB):
                    t0 = b * TBK
                    for j in range(4):
                        a_ = abt.next()
                        dma(a_.ap[:], a_.keys[0], ABd.ap()[j][:, t0:t0 + TBK], ("AB", b))
                        v_ = acvt.next()
                        dma(v_.ap[:], v_.keys[0], ACV.ap()[j][:, t0 + 1:t0 + TBK + 3], [("ACV", b - 1), ("ACV", b), ("ACV", b + 1)])
                        acc = f32r.next()
                        ts("pool", acc.ap[:], acc.keys[0], v_.ap[:, 0:512], v_.keys[0], caw(j, 0), None, ALU.mult, extra_r=[PPK])
                        stt("pool", acc.ap[:], acc.keys[0], v_.ap[:, 1:513], v_.keys[0], caw(j, 1), acc.ap[:], acc.keys[0], ALU.mult, ALU.add, extra_r=[PPK])
                        stt("pool", acc.ap[:], acc.keys[0], v_.ap[:, 2:514], v_.keys[0], caw(j, 2), acc.ap[:], acc.keys[0], ALU.mult, ALU.add, extra_r=[PPK])
                        tt("pool", h16[:, j, :], ("h16", j), acc.ap[:], acc.keys[0], a_.ap[:], a_.keys[0], ALU.mult)
                    u_ = ut.next()
                    for j in range(4):
                        dma(u_.ap[:, j, :], u_.keys[0], Ud.ap()[j][:, t0:t0 + TBK + 30], [("U", b - 1), ("U", b), ("U", b + 1)])
                    for k in range(31):
                        for j in range(4):
                            eng = "dve" if j < 2 else "pool"
                            if k == 0:
                                ts(eng, cacc[:, j, :], ("cacc", j), u_.ap[:, j, 0:512], u_.keys[0], cdw(j, 0), cdb(j), ALU.mult, ALU.add, extra_r=[PPK])
                            else:
                                stt(eng, cacc[:, j, :], ("cacc", j), u_.ap[:, j, k:k + 512], u_.keys[0], cdw(j, k), cacc[:, j, :], ("cacc", j),
                                    ALU.mult, ALU.add, extra_r=[PPK])
                    for j in range(4):
                        mm(pX[:], "pX", ones512[:], "ones512", cacc[:, j, :], ("cacc", j), j == 0, j == 3)
                    for j in range(4):
                        tt("dve", cacc[:, j, :], ("cacc", j), cacc[:, j, :], ("cacc", j), pX[:], "pX", ALU.subtract)
                    sqs = []
                    for j in range(4):
                        sq = f32r.next()
                        act(sq.ap[:], sq.keys[0], cacc[:, j, :], ("cacc", j), AF.Square)
                        sqs.append(sq)
                    for j in range(4):
                        mm(pX[:], "pX", ones512[:], "ones512", sqs[j].ap[:], sqs[j].keys[0], j == 0, j == 3)
                    act(sdt[:], "sdt", pX[:], "pX", AF.Sqrt, bias=EPS5, extra_r=["epst"])
                    recip(rstd[:], "rstd", sdt[:], "sdt")
                    for j in range(4):
                        n_ = f32r.next()
                        tt("dve", n_.ap[:], n_.keys[0], cacc[:, j, :], ("cacc", j), rstd[:], "rstd", ALU.mult)
                        act(h16[:, 12 + j, :], ("h16", 12 + j), n_.ap[:], n_.keys[0], AF.Silu, bias=lnb(j), scale=lng(j), extra_r=[PPK])
                    dma(kcw[0:64, 0, :], "kcw", KCT.ap()[0:64, t0:t0 + 768], [("KCT", b - 1), ("KCT", b), ("KCT", b + 1)])
                    dma(kcw[64:128, 1, :], "kcw", KCT.ap()[64:128, t0:t0 + 768], [("KCT", b - 1), ("KCT", b), ("KCT", b + 1)])
                    dma(vcw[:], "vcw", VCd.ap()[t0:t0 + 768, :].rearrange("(s p) c -> p s c", p=128), [("VC", b - 1), ("VC", b), ("VC", b + 1)])
                    for j in range(4):
                        dma(qcblk[:, j, :], "qcblk", QC.ap()[j][:, t0:t0 + TBK], ("QC", b))
                    for s in range(4):
                        n = 4 * b + s
                        mi = 0 if n == 0 else (2 if n == NC_ - 1 else 1)
                        Rv = Rt.next()
                        for j in range(4):
                            for hf in range(2):
                                h = j + 4 * hf
                                sp_ = pS.next()
                                mm(sp_.ap[:, 0:384], sp_.keys[0], qcblk[:, j, s * 128:(s + 1) * 128], "qcblk",
                                   kcw[:, hf, s * 128:s * 128 + 384], "kcw", True, True)
                                sm = smr.next()
                                tt("dve", sm.ap[:], sm.keys[0], sp_.ap[:, 0:384], sp_.keys[0], mask_s[:, mi, :], "mask_s", ALU.add)
                                sv = smallr.next()
                                P.op("dve", lambda e, o=sv.ap[:, 0:1], i=sm.ap[:]: e.tensor_reduce(out=o, in_=i, axis=AX.X, op=ALU.max),
                                     reads=[sm], writes=[sv])
                                ts("dve", sv.ap[:, 1:2], sv.keys[0], sv.ap[:, 0:1], sv.keys[0], sink(h), -1.0, ALU.max, ALU.mult, extra_r=[PPK])
                                e_ = er.next()
                                act(e_.ap[:], e_.keys[0], sm.ap[:], sm.keys[0], AF.Exp, bias=sv.ap[:, 1:2], accum=sv.ap[:, 2:3],
                                    extra_r=[sv], extra_w=[sv])
                                act(sv.ap[:, 3:4], sv.keys[0], sv.ap[:, 1:2], sv.keys[0], AF.Exp, bias=sink(h), extra_r=[PPK])
                                tt("dve", sv.ap[:, 0:1], sv.keys[0], sv.ap[:, 2:3], sv.keys[0], sv.ap[:, 3:4], sv.keys[0], ALU.add)
                                recip(Rv.ap[:, h:h + 1], Rv.keys[0], sv.ap[:, 0:1], sv.keys[0])
                                pt = pT.next()
                                for c in range(3):
                                    tr(pt.ap[:, c * 128:(c + 1) * 128], pt.keys[0], e_.ap[:, c * 128:(c + 1) * 128], e_.keys[0], ident_b, CMB)
                                et = etr.next()
                                cp("pool" if False else evac_eng(), et.ap[:], et.keys[0], pt.ap[:, 0:384].rearrange("p (a b) -> p a b", a=3), pt.keys[0])
                                for c in range(3):
                                    mm(pO[0][:, h * 64:(h + 1) * 64], "pO0", et.ap[:, c, :], et.keys[0],
                                       vcw[:, s + c, hf * 64:(hf + 1) * 64], "vcw", c == 0, c == 2)
                        ob = osb.next()
                        for h in range(8):
                            if h % 2 == 0:
                                act(ob.ap[:, h * 64:(h + 1) * 64], ob.keys[0], pO[0][:, h * 64:(h + 1) * 64], "pO0", AF.Copy,
                                    scale=Rv.ap[:, h:h + 1], extra_r=[Rv])
                            else:
                                ts("dve", ob.ap[:, h * 64:(h + 1) * 64], ob.keys[0], pO[0][:, h * 64:(h + 1) * 64], "pO0",
                                   Rv.ap[:, h:h + 1], None, ALU.mult, extra_r=[Rv])
                        pt = pT.next()
                        for jj in range(4):
                            tr(pt.ap[:, jj * 128:(jj + 1) * 128], pt.keys[0], ob.ap[:, jj * 128:(jj + 1) * 128], ob.keys[0], ident_b, CMB)
                        cp(evac_eng(), h16[:, 8:12, s * 128:(s + 1) * 128], [("h16", 8), ("h16", 9), ("h16", 10), ("h16", 11)],
                           pt.ap[:, 0:512].rearrange("p (a b) -> p a b", a=4), pt.keys[0])
                    steps = [(j, hf, kc) for j in range(4) for hf in range(2) for kc in range(NC_)]
                    qps = {}
                    pts = {}

                    def rec_S(t):
                        j, hf, kc = steps[t]
                        if hf == 0 and kc == 0:
                            qp = qpad.next()
                            dma(qp.ap[0:64, 0, :], qp.keys[0], QB.ap()[j][0:64, t0:t0 + TBK], ("QB", b))
                            dma(qp.ap[64:128, 1, :], qp.keys[0], QB.ap()[j][64:128, t0:t0 + TBK], ("QB", b))
                            qps[j] = qp
                        qp = qps[j]
                        sp_ = pS.next()
                        mm(sp_.ap[:], sp_.keys[0], KT[:, kc * 128:(kc + 1) * 128], ("KT", kc // 4), qp.ap[:, hf, :], qp.keys[0], True, True)
                        pt_ = b16r.next()
                        act(pt_.ap[:], pt_.keys[0], sp_.ap[:], sp_.keys[0], AF.Exp)
                        pts[t] = pt_

                    rec_S(0)
                    rec_S(1)
                    vtile = None
                    for t in range(len(steps)):
                      j, hf, kc = steps[t]
                      if t + 2 < len(steps):
                          rec_S(t + 2)
                      if kc % VG == 0:
                          vtile = vring.next()
                          dma(vtile.ap[:, 0:VG, :], vtile.keys[0], VAUG.ap()[hf, kc:kc + VG].rearrange("c k v -> k c v"),
                              [("VAUG", bb) for bb in range(kc // 4, (kc + VG + 3) // 4)])
                      pt_ = pts.pop(t)
                      mm(pO[hf][:], "pO%d" % hf, vtile.ap[:, kc % VG, :], vtile.keys[0],
                         pt_.ap[:], pt_.keys[0], kc == 0, kc == NC_ - 1)
                      if hf == 1 and kc == NC_ - 1:
                        xr = f32r.next()
                        recip(xr.ap[64:128, :], xr.keys[0], pO[0][64:128, :], "pO0")
                        recip(xr.ap[0:64, :], xr.keys[0], pO[1][0:64, :], "pO1")
                        mm(pX[:], "pX", swap_f, CM, xr.ap[:], xr.keys[0], True, True)
                        sw = f32r.next()
                        cp("act", sw.ap[:], sw.keys[0], pX[:], "pX")
                        tt("dve", h16[0:64, 4 + j, :], ("h16", 4 + j), pO[0][0:64, :], "pO0", sw.ap[0:64, :], sw.keys[0], ALU.mult)
                        tt("dve", h16[64:128, 4 + j, :], ("h16", 4 + j), pO[1][64:128, :], "pO1", sw.ap[64:128, :], sw.keys[0], ALU.mult)
                    for oc in range(16):
                        wt = wA.next()
                        dma(wt.ap[:], wt.keys[0], WOUT.ap()[l, oc].rearrange("p (a b) -> p a b", a=16), ("WOUT", l, oc))
                        z = pS.next()
                        for kc in range(16):
                            mm(z.ap[:], z.keys[0], wt.ap[:, kc, :], wt.keys[0], h16[:, kc, :], ("h16", kc), kc == 0, kc == 15)
                        cp(evac_eng(), big32[:, oc, :], ("big32", oc // 4), z.ap[:], z.keys[0])
                    rms_stats(lambda kc: big32[:, kc, :], [("big32", g) for g in range(4)])
                    for g8 in range(8):
                        g = g8 // 2
                        xr_ = xg.next()
                        dma(xr_.ap[:], xr_.keys[0], XT.ap()[2 * g8:2 * g8 + 2, :, t0:t0 + TBK].rearrange("k p t -> p k t"), ("XT", b))
                        for q in range(2):
                            kc = 2 * g8 + q
                            stt("dve", big32[:, kc, :], ("big32", g), big32[:, kc, :], ("big32", g), gpost(kc),
                                rstd[:], "rstd", ALU.mult, ALU.mult, extra_r=[PPK])
                        tt("dve", xr_.ap[:], xr_.keys[0], xr_.ap[:], xr_.keys[0], big32[:, 2 * g8:2 * g8 + 2, :], ("big32", g), ALU.add)
                        dma(X1T.ap()[2 * g8:2 * g8 + 2, :, t0:t0 + TBK].rearrange("k p t -> p k t"), ("X1T", b), xr_.ap[:], xr_.keys[0], eng="pool")
                P.barrier()

                for b in range(NB):
                    t0 = b * TBK
                    for g in range(4):
                        dma(big32[:, 4 * g:4 * g + 4, :], ("big32", g),
                            X1T.ap()[4 * g:4 * g + 4, :, t0:t0 + TBK].rearrange("k p t -> p k t"), ("X1T", b))
                    rms_stats(lambda kc: big32[:, kc, :], [("big32", g) for g in range(4)])
                    for kc in range(16):
                        eng = "dve" if kc % 2 == 0 else "pool"
                        stt(eng, h16[:, kc, :], ("h16", kc), big32[:, kc, :], ("big32", kc // 4), gffn(kc), rstd[:], "rstd",
                            ALU.mult, ALU.mult, extra_r=[PPK])
                    for fc in range(64):
                        wt = wA.next()
                        dma(wt.ap[:], wt.keys[0], W1.ap()[l, fc].rearrange("p (a b) -> p a b", a=16), ("W1", l, fc))
                        z = pS.next()
                        for kc in range(16):
                            mm(z.ap[:], z.keys[0], wt.ap[:, kc, :], wt.keys[0], h16[:, kc, :], ("h16", kc), kc == 0, kc == 15)
                        r_ = f32r.next()
                        act(r_.ap[:], r_.keys[0], z.ap[:], z.keys[0], AF.Relu)
                        tt("dve" if fc % 2 == 0 else "pool", uT[:, fc, :], ("uT", fc), r_.ap[:], r_.keys[0], r_.ap[:], r_.keys[0], ALU.mult)
                    for oc in range(16):
                        z = pS.next()
                        for hh in range(4):
                            wt = wA.next()
                            dma(wt.ap[:], wt.keys[0], W2.ap()[l, oc].rearrange("p (a b) -> p a b", a=64)[:, 16 * hh:16 * hh + 16, :], ("W2", l, oc))
                            for q in range(16):
                                fc = 16 * hh + q
                                mm(z.ap[:], z.keys[0], wt.ap[:, q, :], wt.keys[0], uT[:, fc, :], ("uT", fc), fc == 0, fc == 63)
                        cp(evac_eng(), big32[:, oc, :], ("big32", oc // 4), z.ap[:], z.keys[0])
                    rms_stats(lambda kc: big32[:, kc, :], [("big32", g) for g in range(4)])
                    for g8 in range(8):
                        g = g8 // 2
                        xr_ = xg.next()
                        dma(xr_.ap[:], xr_.keys[0], X1T.ap()[2 * g8:2 * g8 + 2, :, t0:t0 + TBK].rearrange("k p t -> p k t"), ("X1T", b))
                        for q in range(2):
                            kc = 2 * g8 + q
                            stt("dve", big32[:, kc, :], ("big32", g), big32[:, kc, :], ("big32", g), gpffn(kc),
                                rstd[:], "rstd", ALU.mult, ALU.mult, extra_r=[PPK])
                        if l < NL - 1:
                            tt("dve", xr_.ap[:], xr_.keys[0], xr_.ap[:], xr_.keys[0], big32[:, 2 * g8:2 * g8 + 2, :], ("big32", g), ALU.add)
                            dma(XT.ap()[2 * g8:2 * g8 + 2, :, t0:t0 + TBK].rearrange("k p t -> p k t"), ("XT", b), xr_.ap[:], xr_.keys[0], eng="pool")
                        else:
                            tt("dve", big32[:, 2 * g8:2 * g8 + 2, :], ("big32", g), xr_.ap[:], xr_.keys[0], big32[:, 2 * g8:2 * g8 + 2, :], ("big32", g), ALU.add)
                    if l == NL - 1:
                        for s in range(4):
                          for hx in range(2):
                            ot_ = xg.next()
                            ot = TV(ot_.ap.rearrange("p a b -> p (a b)"), ot_.keys)
                            for g2 in range(2):
                                g = 2 * hx + g2
                                pt = pS.next()
                                for q in range(4):
                                    kc = 4 * g + q
                                    tr(pt.ap[:, q * 128:(q + 1) * 128], pt.keys[0], big32[:, kc, s * 128:(s + 1) * 128], ("big32", g), ident_f, CM)
                                cp(evac_eng(), ot.ap[:, 512 * g2:512 * (g2 + 1)], ot.keys[0], pt.ap[:], pt.keys[0])
                            dma(yout[si].ap()[t0 + s * 128:t0 + (s + 1) * 128, 1024 * hx:1024 * (hx + 1)], ("yout", si), ot.ap[:], ot.keys[0], eng="pool")
                P.barrier()

        P.op("sp", None, reads=[("yout", si) for si in range(len(SEGS))], kind="w")
        P.emit()
    return nc


_CACHE = {}


def run_segments(inp, xsegs, SEGS):
    key = tuple(SEGS)
    if key not in _CACHE:
        _CACHE[key] = build_program(SEGS)
    nc = _CACHE[key]
    cm, mask = const_mats()
    pp = pack_params(inp)
    tabs = [rope_tables(S) for S in SEGS]
    base = {"w_in": np.ascontiguousarray(inp["w_in"], np.float32), "w_out": np.ascontiguousarray(inp["w_out"], np.float32),
            "w_ff1": np.ascontiguousarray(inp["w_ff1"], np.float32), "w_ff2": np.ascontiguousarray(inp["w_ff2"], np.float32),
            "cm": cm, "mask": mask, "pp": pp}
    for si in range(len(SEGS)):
        for k in range(4):
            base["tab%d_%d" % (si, k)] = tabs[si][k]
    in_maps = []
    ncores = len(xsegs)
    for c in range(ncores):
        m = dict(base)
        for si in range(len(SEGS)):
            m["x%d" % si] = xsegs[c][si]
        in_maps.append(m)
    res = run_bass_kernel_spmd(nc, in_maps, core_ids=list(range(ncores)))
    return [[res.results[c]["y%d" % si] for si in range(len(SEGS))] for c in range(ncores)]


def kernel(**inputs):
    inp = {k: np.asarray(v) for k, v in inputs.items()}
    xp = np.asarray(inp["x_prompt"], np.float32)
    xs = np.asarray(inp["x_sample"], np.float32)
    SP, SS = xp.shape[1], xs.shape[1]
    xsegs = []
    for c in range(2):
        xsegs.append([np.ascontiguousarray(xp[c]), np.ascontiguousarray(xs[2 * c]), np.ascontiguousarray(xs[2 * c + 1])])
    outs = run_segments(inp, xsegs, [SP, SS, SS])
    yp = np.stack([outs[c][0] for c in range(2)], 0).astype(np.float32)
    ys = np.stack([outs[c // 2][1 + c % 2] for c in range(4)], 0).astype(np.float32)
    return (yp, ys)
```

```python
import bisect
import contextlib
import numpy as np
import ml_dtypes
import concourse.bass as bass
import concourse.mybir as mybir
from concourse.bass_utils import run_bass_kernel_spmd

F32 = mybir.dt.float32
BF16 = mybir.dt.bfloat16
ALU = mybir.AluOpType
AF = mybir.ActivationFunctionType
AX = mybir.AxisListType

D = 2048
INW = 4096
DFF = 8192
NL = 2
TBK = 512
ENGS = ["pe", "act", "dve", "pool", "sp"]
N_DMA_SEMS = 16


class TV:
    __slots__ = ("ap", "keys")

    def __init__(self, ap, keys):
        self.ap = ap
        self.keys = tuple(keys) if isinstance(keys, (list, tuple)) else (keys,)


class Op:
    __slots__ = ("eng", "fn", "reads", "writes", "kind", "idx", "slot", "slot_n",
                 "waits", "target", "clock")

    def __init__(self, eng, fn, reads, writes, kind):
        self.eng = eng
        self.fn = fn
        self.reads = reads
        self.writes = writes
        self.kind = kind
        self.waits = []
        self.target = False
        self.idx = None
        self.slot = None
        self.slot_n = None
        self.clock = None


def _keys(lst):
    out = []
    for r in lst:
        if isinstance(r, TV):
            out.extend(r.keys)
        elif isinstance(r, list):
            out.extend(_keys(r))
        else:
            out.append(r)
    return tuple(out)


class Prog:
    def __init__(self, nc):
        self.nc = nc
        self.ops = []

    def op(self, eng, fn, reads=(), writes=(), kind="c"):
        o = Op(eng, fn, _keys(reads), _keys(writes), kind)
        self.ops.append(o)
        return o

    def barrier(self):
        for e in ENGS:
            self.ops.append(Op(e, None, (), (), "b"))

    def analyze(self):
        last_w = {}
        readers = {}
        eng_count = {e: 0 for e in ENGS}
        eng_last = {e: None for e in ENGS}
        seen = {e: {} for e in ENGS}
        slot_last = [None] * N_DMA_SEMS
        slot_cnt = [0] * N_DMA_SEMS
        pools = {"sp": list(range(0, 10)), "pool": list(range(10, N_DMA_SEMS)), "act": list(range(10, N_DMA_SEMS))}
        rr = {"sp": 0, "pool": 0, "act": 0}
        for o in self.ops:
            deps = []
            if o.kind == "b":
                for e in ENGS:
                    if eng_last[e] is not None:
                        deps.append(eng_last[e])
                for s in slot_last:
                    if s is not None:
                        deps.append(s)
            else:
                for k in o.reads:
                    w = last_w.get(k)
                    if w is not None:
                        deps.append(w)
                for k in o.writes:
                    w = last_w.get(k)
                    if w is not None:
                        deps.append(w)
                    rs = readers.get(k)
                    if rs:
                        deps.extend(rs)
            if o.kind == "d":
                pl = pools[o.eng]
                s = pl[rr[o.eng] % len(pl)]
                rr[o.eng] += 1
                o.slot = s
                if slot_last[s] is not None:
                    deps.append(slot_last[s])
                slot_cnt[s] += 1
                o.slot_n = slot_cnt[s]
                slot_last[s] = o
            E = o.eng
            sE = seen[E]
            for p in deps:
                if p is o:
                    continue
                if p.kind == "d":
                    key = ("s", p.slot)
                    val = p.slot_n
                elif p.kind == "c":
                    if p.eng == "pe" and E == "pe" and o.kind != "b":
                        continue
                    if p.eng == E and o.kind == "b":
                        continue
                    key = p.eng
                    val = p.idx
                else:
                    continue
                if sE.get(key, 0) >= val:
                    continue
                o.waits.append((key, val))
                p.target = True
                for kk, vv in p.clock.items():
                    if sE.get(kk, 0) < vv:
                        sE[kk] = vv
                sE[key] = val
            if o.kind == "c":
                eng_count[E] += 1
                o.idx = eng_count[E]
                eng_last[E] = o
            c = dict(sE)
            if o.kind == "c":
                c[E] = o.idx
            elif o.kind == "d":
                c[("s", o.slot)] = o.slot_n
            o.clock = c
            if o.kind in ("c", "d"):
                for k in o.reads:
                    readers.setdefault(k, []).append(o)
                for k in o.writes:
                    last_w[k] = o
                    readers[k] = []
        self.tidx = {e: [] for e in ENGS}
        for o in self.ops:
            if o.kind == "c" and o.target:
                self.tidx[o.eng].append(o.idx)
        for o in self.ops:
            o.clock = None

    def emit(self):
        nc = self.nc
        self.analyze()
        with contextlib.ExitStack() as es:
            esem = {e: es.enter_context(nc.semaphore("sem_" + e)) for e in ENGS}
            dsem = [es.enter_context(nc.semaphore("dsem%d" % i)) for i in range(N_DMA_SEMS)]
            block = es.enter_context(nc.Block())
            per = {e: [o for o in self.ops if o.eng == e] for e in ENGS}
            tidx = self.tidx

            def run(eng_name, e):
                for o in per[eng_name]:
                    ws = []
                    for key, val in o.waits:
                        if isinstance(key, tuple):
                            ws.append((dsem[key[1]], 16 * val))
                        else:
                            ws.append((esem[key], bisect.bisect_right(tidx[key], val)))
                    if o.kind in ("w", "b"):
                        for sm_, v_ in ws:
                            e.wait_ge(sm_, v_)
                        continue
                    for sm_, v_ in ws[:-1]:
                        e.wait_ge(sm_, v_)
                    ins = o.fn(e)
                    if ws:
                        ins._wait_ge(ws[-1][0], ws[-1][1])
                    if o.kind == "c":
                        if o.target:
                            ins.then_inc(esem[eng_name], 1)
                    else:
                        ins.then_inc(dsem[o.slot], 16)

            @block.tensor
            def _(e):
                run("pe", e)

            @block.scalar
            def _(e):
                run("act", e)

            @block.vector
            def _(e):
                run("dve", e)

            @block.gpsimd
            def _(e):
                run("pool", e)

            @block.sync
            def _(e):
                run("sp", e)


def win_chunks():
    ch = []
    for j in range(4):
        ch.append(("ab", j, [(128 * j, 128, 0)]))
    for j in range(4):
        ch.append(("ac", j, [(512 + 128 * j, 128, 0)]))
        ch.append(("av", j, [(1024 + 128 * j, 128, 0)]))
    for j in range(4):
        ch.append(("dg", j, [(3584 + 128 * j, 128, 0)]))
        ch.append(("da", j, [(3072 + 128 * j, 128, 0)]))
    for j in range(4):
        ch.append(("qb", j, [(1536 + 64 * j, 64, 0), (1536 + 64 * (4 + j), 64, 64)]))
    ch.append(("kb", 0, [(2048, 128, 0)]))
    ch.append(("vb", 0, [(2176, 128, 0)]))
    for j in range(4):
        ch.append(("qc", j, [(2304 + 64 * j, 64, 0), (2304 + 64 * (4 + j), 64, 64)]))
    ch.append(("kc", 0, [(2816, 128, 0)]))
    ch.append(("vc", 0, [(2944, 128, 0)]))
    return ch


def rope_tables(S):
    theta = np.float32(10000.0)
    t = np.arange(S)
    inv16 = (np.float32(1.0) / (theta ** (np.arange(0, 32, 2, dtype=np.float32) / np.float32(32)))).astype(np.float32)
    inv32 = (np.float32(1.0) / (theta ** (np.arange(0, 64, 2, dtype=np.float32) / np.float32(64)))).astype(np.float32)
    rowp = (t // 64).astype(np.float32)
    colp = (t % 64).astype(np.float32)
    linp = t.astype(np.float32)
    cb = np.zeros((128, S), np.float32)
    sb_ = np.zeros((128, S), np.float32)
    cc = np.zeros((128, S), np.float32)
    sc = np.zeros((128, S), np.float32)
    for p in range(128):
        d = p % 64
        pos = rowp if d < 32 else colp
        ang = (pos * inv16[d % 16]).astype(np.float32)
        cb[p] = np.cos(ang)
        sb_[p] = np.sin(ang)
        ang2 = (linp * inv32[d % 32]).astype(np.float32)
        cc[p] = np.cos(ang2)
        sc[p] = np.sin(ang2)
    return cb, sb_, cc, sc


def const_mats():
    ident = np.eye(128, dtype=np.float32)
    rotb = np.zeros((128, 128), np.float32)
    rotc = np.zeros((128, 128), np.float32)
    for m in range(128):
        if m % 32 < 16:
            rotb[m + 16, m] = -1.0
        else:
            rotb[m - 16, m] = 1.0
        if m % 64 < 32:
            rotc[m + 32, m] = -1.0
        else:
            rotc[m - 32, m] = 1.0
    swap = np.zeros((128, 128), np.float32)
    blk = np.zeros((128, 128), np.float32)
    for i in range(64):
        swap[i + 64, i] = 1.0
        swap[i, i + 64] = 1.0
    blk[:64, :64] = 1.0 / 64
    blk[64:, 64:] = 1.0 / 64
    mask = np.zeros((128, 3, 384), np.float32)
    i = np.arange(128)[:, None]
    jj = np.arange(384)[None, :]
    band = np.abs(i - (jj - 128)) <= 128
    mask[:, 1, :] = np.where(band, 0.0, -1e30)
    mask[:, 0, :] = np.where(band & (jj >= 128), 0.0, -1e30)
    mask[:, 2, :] = np.where(band & (jj < 256), 0.0, -1e30)
    cm = np.concatenate([ident, rotb, rotc, swap, blk], axis=1)
    return cm, mask.reshape(128, 3 * 384)


PP_GPRE, PP_GPOST, PP_GFFN, PP_GPFFN = 0, 32, 64, 96
PP_GQ, PP_GK, PP_SINK, PP_CAW, PP_CDW, PP_CDB, PP_LNG, PP_LNB = 128, 130, 132, 148, 172, 420, 428, 436
PP_N = 444


def pack_params(inp):
    pp = np.zeros((128, PP_N), np.float32)
    for l in range(NL):
        pp[:, PP_GPRE + 16 * l:PP_GPRE + 16 * l + 16] = inp["pre_mix_g"][l].reshape(16, 128).T
        pp[:, PP_GPOST + 16 * l:PP_GPOST + 16 * l + 16] = inp["post_mix_g"][l].reshape(16, 128).T
        pp[:, PP_GFFN + 16 * l:PP_GFFN + 16 * l + 16] = inp["pre_ffn_g"][l].reshape(16, 128).T
        pp[:, PP_GPFFN + 16 * l:PP_GPFFN + 16 * l + 16] = inp["post_ffn_g"][l].reshape(16, 128).T
        pp[:, PP_GQ + l] = np.tile(inp["q_norm_g"][l], 2)
        pp[:, PP_GK + l] = np.tile(inp["k_norm_g"][l], 2)
        pp[:, PP_SINK + 8 * l:PP_SINK + 8 * l + 8] = inp["sink_c"][l][None, :]
        for j in range(4):
            pp[:, PP_CAW + 12 * l + 3 * j:PP_CAW + 12 * l + 3 * j + 3] = inp["conv_a_w"][l][:, 128 * j:128 * j + 128].T
            pp[:, PP_CDW + 124 * l + 31 * j:PP_CDW + 124 * l + 31 * j + 31] = inp["conv_d_w"][l][:, 128 * j:128 * j + 128].T
            pp[:, PP_CDB + 4 * l + j] = inp["conv_d_b"][l][128 * j:128 * j + 128]
            pp[:, PP_LNG + 4 * l + j] = inp["ln_d_g"][l][128 * j:128 * j + 128]
            pp[:, PP_LNB + 4 * l + j] = inp["ln_d_b"][l][128 * j:128 * j + 128]
    return pp


def build_program(SEGS):
    nc = bass.Bass("TRN2", target_bir_lowering=False)
    P = Prog(nc)
    WCH = win_chunks()

    def din(name, shape, dt=F32):
        return nc.dram_tensor(name, list(shape), dt, kind="ExternalInput")

    w_in = din("w_in", [NL, D, INW])
    w_out = din("w_out", [NL, D, D])
    w_ff1 = din("w_ff1", [NL, D, DFF])
    w_ff2 = din("w_ff2", [NL, DFF, D])
    cm_d = din("cm", [128, 640])
    mask_d = din("mask", [128, 3 * 384])
    pp_d = din("pp", [128, PP_N])
    xin, yout, tabs = [], [], []
    for si, S in enumerate(SEGS):
        xin.append(din("x%d" % si, [S, D]))
        yout.append(nc.dram_tensor("y%d" % si, [S, D], F32, kind="ExternalOutput"))
        tabs.append([din("tab%d_%d" % (si, k), [128, S]) for k in range(4)])
    SM = max(SEGS)
    WIN = nc.dram_tensor("WIN", [NL, 32, 128, 2048], BF16)
    WOUT = nc.dram_tensor("WOUT", [NL, 16, 128, 2048], BF16)
    W1 = nc.dram_tensor("W1", [NL, 64, 128, 2048], BF16)
    W2 = nc.dram_tensor("W2", [NL, 16, 128, 8192], BF16)
    XT = nc.dram_tensor("XT", [16, 128, SM], F32)
    X1T = nc.dram_tensor("X1T", [16, 128, SM], F32)
    QB = nc.dram_tensor("QB", [4, 128, SM], BF16)
    QC = nc.dram_tensor("QC", [4, 128, SM], BF16)
    ABd = nc.dram_tensor("ABd", [4, 128, SM], BF16)
    ACV = nc.dram_tensor("ACV", [4, 128, SM + 4], BF16)
    Ud = nc.dram_tensor("Ud", [4, 128, SM + 30], BF16)
    KBT = nc.dram_tensor("KBT", [128, SM], BF16)
    VBd = nc.dram_tensor("VBd", [SM, 128], BF16)
    KCT = nc.dram_tensor("KCT", [128, SM + 256], BF16)
    VCd = nc.dram_tensor("VCd", [SM + 256, 128], BF16)
    VAUG = nc.dram_tensor("VAUG", [2, SM // 128, 128, 128], BF16)

    with contextlib.ExitStack() as es:
        def sb(name, shape, dt):
            return es.enter_context(nc.sbuf_tensor(name, list(shape), dt))

        def ps(name, shape, dt):
            return es.enter_context(nc.psum_tensor(name, list(shape), dt))

        class Ring:
            def __init__(self, name, n, shape, dt, maker=sb):
                self.t = [maker("%s%d" % (name, i), shape, dt) for i in range(n)]
                self.name = name
                self.i = 0

            def next(self):
                k = self.i % len(self.t)
                self.i += 1
                return TV(self.t[k], [(self.name, k)])

        big32 = sb("big32", [128, 16, 512], F32)
        h16 = sb("h16", [128, 16, 512], BF16)
        R80 = sb("R80", [128, 32768], BF16)
        wA = Ring("wA", 6, [128, 16, 128], BF16)
        cm_s = sb("cm_s", [128, 640], F32)
        cmb_s = sb("cmb_s", [128, 640], BF16)
        mask_s = sb("mask_s", [128, 3, 384], F32)
        pp_s = sb("pp_s", [128, PP_N], F32)
        onesb = sb("onesb", [128, 128], BF16)
        ones512 = sb("ones512", [128, 128], F32)
        epst = sb("epst", [128, 4], F32)
        rstd = sb("rstd", [128, 512], F32)
        sdt = sb("sdt", [128, 512], F32)
        sqr = Ring("sqr", 2, [128, 512], BF16)
        f32r = Ring("f32r", 4, [128, 512], F32)
        b16r = Ring("b16r", 4, [128, 512], BF16)
        xg = Ring("xg", 2, [128, 2, 512], F32)
        qpad = Ring("qpad", 2, [128, 2, 512], BF16)
        kcw = sb("kcw", [128, 2, 768], BF16)
        vcw = sb("vcw", [128, 6, 128], BF16)
        qcblk = R80[:, 22656:24704].rearrange("p (a b) -> p a b", a=4)
        vring = Ring("vring", 3, [128, 8, 128], BF16)
        vaA = Ring("vaA", 2, [128, 4, 128], BF16)
        vaB = Ring("vaB", 2, [128, 4, 128], BF16)
        smr = Ring("smr", 2, [128, 384], F32)
        er = Ring("er", 2, [128, 384], BF16)
        etr = Ring("etr", 2, [128, 3, 128], BF16)
        smallr = Ring("smallr", 8, [128, 4], F32)
        Rt = Ring("Rt", 2, [128, 8], F32)
        osb = Ring("osb", 2, [128, 512], BF16)
        abt = Ring("abt", 2, [128, 512], BF16)
        acvt = Ring("acvt", 2, [128, 514], BF16)
        class UtRing:
            def next(self):
                return TV(R80[:, 20480:22648].rearrange("p (a b) -> p a b", a=4), [("ut", 0)])
        ut = UtRing()
        cacc = R80[:, 16384:20480].bitcast(F32).rearrange("p (a b) -> p a b", a=4)
        vt16 = Ring("vt16", 2, [128, 4, 128], BF16)

        class TabRing:
            def __init__(self):
                self.i = 0

            def next(self):
                k = self.i % 4
                self.i += 1
                return TV(cacc[:, k, :], [("cacc", k)])
        tabr = TabRing()

        pS = Ring("pS", 3, [128, 512], F32, maker=ps)
        pO = [ps("pO%d" % i, [128, 512], F32) for i in range(2)]
        pX = ps("pX", [128, 512], F32)
        pT = Ring("pT", 2, [128, 1024], BF16, maker=ps)

        ident_f = cm_s[:, 0:128]
        rotb_f = cm_s[:, 128:256]
        rotc_f = cm_s[:, 256:384]
        swap_f = cm_s[:, 384:512]
        ident_b = cmb_s[:, 0:128]
        blk_b = cmb_s[:, 512:640]
        CM = "cm_s"
        CMB = "cmb_s"
        PPK = "pp_s"

        cnt = {"ld": 0, "ev": 0}

        def dma(out_ap, out_k, in_ap, in_k, eng="sp"):
            P.op(eng, lambda e: e.dma_start(out=out_ap, in_=in_ap), reads=[in_k] if not isinstance(in_k, list) else in_k,
                 writes=[out_k] if not isinstance(out_k, list) else out_k, kind="d")

        def mm(out_ap, out_k, lhsT, lk, rhs, rk, start, stop):
            P.op("pe", lambda e: e.matmul(out_ap, lhsT, rhs, start=start, stop=stop), reads=[lk, rk], writes=[out_k])

        def tr(out_ap, out_k, in_ap, in_k, idt, idk):
            P.op("pe", lambda e: e.transpose(out_ap, in_ap, idt), reads=[in_k, idk], writes=[out_k])

        def act(out_ap, out_k, in_ap, in_k, func, bias=None, scale=None, accum=None, extra_r=(), extra_w=()):
            def f(e):
                kw = {}
                if bias is not None:
                    kw["bias"] = bias
                if scale is not None:
                    kw["scale"] = scale
                if accum is not None:
                    kw["accum_out"] = accum
                return e.activation(out=out_ap, in_=in_ap, func=func, **kw)
            P.op("act", f, reads=[in_k] + list(extra_r), writes=[out_k] + list(extra_w))

        def tt(eng, out_ap, out_k, a, ak, b, bk, op):
            P.op(eng, lambda e: e.tensor_tensor(out=out_ap, in0=a, in1=b, op=op), reads=[ak, bk], writes=[out_k])

        def ts(eng, out_ap, out_k, a, ak, s1, s2, op0, op1=None, extra_r=()):
            eng = "dve"

            def f(e):
                if op1 is None:
                    return e.tensor_scalar(out=out_ap, in0=a, scalar1=s1, scalar2=None, op0=op0)
                return e.tensor_scalar(out=out_ap, in0=a, scalar1=s1, scalar2=s2, op0=op0, op1=op1)
            P.op(eng, f, reads=[ak] + list(extra_r), writes=[out_k])

        def stt(eng, out_ap, out_k, a, ak, s, b, bk, op0, op1, extra_r=()):
            eng = "dve"
            P.op(eng, lambda e: e.scalar_tensor_tensor(out=out_ap, in0=a, scalar=s, in1=b, op0=op0, op1=op1),
                 reads=[ak, bk] + list(extra_r), writes=[out_k])

        def cp(eng, out_ap, out_k, in_ap, in_k):
            if eng == "act":
                P.op("act", lambda e: e.copy(out=out_ap, in_=in_ap), reads=[in_k], writes=[out_k])
            else:
                P.op(eng, lambda e: e.tensor_copy(out=out_ap, in_=in_ap), reads=[in_k], writes=[out_k])

        def recip(out_ap, out_k, in_ap, in_k):
            P.op("dve", lambda e: e.reciprocal(out=out_ap, in_=in_ap), reads=[in_k], writes=[out_k])

        def memset(eng, ap, k, v):
            P.op(eng, lambda e: e.memset(ap, v), writes=[k])

        def evac_eng():
            cnt["ev"] += 1
            return "act" if cnt["ev"] % 2 else "dve"

        dma(cm_s[:], CM, cm_d.ap(), "cm_d")
        dma(mask_s[:], "mask_s", mask_d.ap().rearrange("p (a b) -> p a b", a=3), "mask_d")
        dma(pp_s[:], PPK, pp_d.ap(), "pp_d")
        cp("dve", cmb_s[:], CMB, cm_s[:], CM)
        memset("dve", onesb[:], "onesb", 1.0 / 2048)
        memset("dve", ones512[:], "ones512", 1.0 / 512)
        memset("dve", epst[:, 0:1], "epst", 1e-6)
        memset("dve", epst[:, 1:2], "epst", 64e-6)
        memset("dve", epst[:, 2:3], "epst", 1e-5)
        memset("dve", epst[:, 3:4], "epst", 0.0)
        EPS6 = epst[:, 0:1]
        EPS6x64 = epst[:, 1:2]
        EPS5 = epst[:, 2:3]

        st32 = [R80[:, 4096 * i:4096 * (i + 1)].bitcast(F32).rearrange("p (a b) -> p a b", a=16) for i in range(4)]
        st16 = [R80[:, 16384 + 2048 * i:16384 + 2048 * (i + 1)].rearrange("p (a b) -> p a b", a=16) for i in range(4)]
        pc = [0]

        def prep_tile(src_list, dst_ap, dst_k):
            i = pc[0] % 4
            pc[0] += 1
            s32, s16 = st32[i], st16[i]
            for n_, (sap, dap) in enumerate(src_list):
                dma(dap(s32), ("st32", i, n_), sap, "wsrc")
            eng = ["dve", "pool", "act"][pc[0] % 3]
            cp(eng, s16, ("st16", i), s32, [("st32", i, n_) for n_ in range(10)])
            dma(dst_ap, dst_k, s16, ("st16", i), eng="pool")

        for l in range(NL):
            wv = w_in.ap()[l].rearrange("(kc p) n -> p kc n", p=128)
            for oc, (typ, j, segs) in enumerate(WCH):
                srcs = []
                for (c0, n, off) in segs:
                    srcs.append((wv[:, :, c0:c0 + n], (lambda t, off=off, n=n: t[:, :, off:off + n])))
                prep_tile(srcs, WIN.ap()[l, oc].rearrange("p (a b) -> p a b", a=16), ("WIN", l, oc))
            wv = w_out.ap()[l].rearrange("(kc p) n -> p kc n", p=128)
            wrow = w_out.ap()[l]
            for oc in range(16):
                srcs = [(wv[:, 0:4, oc * 128:(oc + 1) * 128], (lambda t: t[:, 0:4, :])),
                        (wv[:, 8:16, oc * 128:(oc + 1) * 128], (lambda t: t[:, 8:16, :]))]
                for j in range(4):
                    for hf in range(2):
                        r0 = 512 + 64 * (j + 4 * hf)
                        srcs.append((wrow[r0:r0 + 64, oc * 128:(oc + 1) * 128],
                                     (lambda t, j=j, hf=hf: t[64 * hf:64 * hf + 64, 4 + j, :])))
                prep_tile(srcs, WOUT.ap()[l, oc].rearrange("p (a b) -> p a b", a=16), ("WOUT", l, oc))
            wv = w_ff1.ap()[l].rearrange("(kc p) n -> p kc n", p=128)
            for oc in range(64):
                prep_tile([(wv[:, :, oc * 128:(oc + 1) * 128], (lambda t: t))],
                          W1.ap()[l, oc].rearrange("p (a b) -> p a b", a=16), ("W1", l, oc))
            wv = w_ff2.ap()[l].rearrange("(kc p) n -> p kc n", p=128)
            for oc in range(16):
                for q in range(4):
                    prep_tile([(wv[:, 16 * q:16 * q + 16, oc * 128:(oc + 1) * 128], (lambda t: t))],
                              W2.ap()[l, oc].rearrange("p (a b) -> p a b", a=64)[:, 16 * q:16 * q + 16, :], ("W2", l, oc))
        P.barrier()

        def rms_stats(src_fn, src_k):
            for kc in range(16):
                sq = sqr.next()
                act(sq.ap[:], sq.keys[0], src_fn(kc), src_k, AF.Square)
                mm(pX[:], "pX", onesb[:], "onesb", sq.ap[:], sq.keys[0], kc == 0, kc == 15)
            act(sdt[:], "sdt", pX[:], "pX", AF.Sqrt, bias=EPS6, extra_r=["epst"])
            recip(rstd[:], "rstd", sdt[:], "sdt")

        for si, S in enumerate(SEGS):
            NB = S // TBK
            NC_ = S // 128
            TAB = tabs[si]
            KT = R80[:, 0:S]
            VG = min(8, NC_)
            uT = R80[:, 0:32768].rearrange("p (a b) -> p a b", a=64)
            zt = b16r.next()
            memset("dve", zt.ap[:], zt.keys[0], 0.0)
            for j in range(4):
                dma(ACV.ap()[j][:, 0:2], ("ACV", -1), zt.ap[:, 0:2], zt.keys[0])
                dma(ACV.ap()[j][:, S + 2:S + 4], ("ACV", NB), zt.ap[:, 0:2], zt.keys[0])
                dma(Ud.ap()[j][:, 0:15], ("U", -1), zt.ap[:, 0:15], zt.keys[0])
                dma(Ud.ap()[j][:, S + 15:S + 30], ("U", NB), zt.ap[:, 0:15], zt.keys[0])
            dma(KCT.ap()[:, 0:128], ("KCT", -1), zt.ap[:, 0:128], zt.keys[0])
            dma(KCT.ap()[:, S + 128:S + 256], ("KCT", NB), zt.ap[:, 0:128], zt.keys[0])
            dma(VCd.ap()[0:128, :], ("VC", -1), zt.ap[:, 0:128], zt.keys[0])
            dma(VCd.ap()[S + 128:S + 256, :], ("VC", NB), zt.ap[:, 0:128], zt.keys[0])

            for l in range(NL):
                gpre = lambda kc: pp_s[:, PP_GPRE + 16 * l + kc:PP_GPRE + 16 * l + kc + 1]
                gpost = lambda kc: pp_s[:, PP_GPOST + 16 * l + kc:PP_GPOST + 16 * l + kc + 1]
                gffn = lambda kc: pp_s[:, PP_GFFN + 16 * l + kc:PP_GFFN + 16 * l + kc + 1]
                gpffn = lambda kc: pp_s[:, PP_GPFFN + 16 * l + kc:PP_GPFFN + 16 * l + kc + 1]
                gq = pp_s[:, PP_GQ + l:PP_GQ + l + 1]
                gk = pp_s[:, PP_GK + l:PP_GK + l + 1]
                sink = lambda h: pp_s[:, PP_SINK + 8 * l + h:PP_SINK + 8 * l + h + 1]
                caw = lambda j, k: pp_s[:, PP_CAW + 12 * l + 3 * j + k:PP_CAW + 12 * l + 3 * j + k + 1]
                cdw = lambda j, k: pp_s[:, PP_CDW + 124 * l + 31 * j + k:PP_CDW + 124 * l + 31 * j + k + 1]
                cdb = lambda j: pp_s[:, PP_CDB + 4 * l + j:PP_CDB + 4 * l + j + 1]
                lng = lambda j: pp_s[:, PP_LNG + 4 * l + j:PP_LNG + 4 * l + j + 1]
                lnb = lambda j: pp_s[:, PP_LNB + 4 * l + j:PP_LNB + 4 * l + j + 1]

                for b in range(NB):
                    t0 = b * TBK
                    if l == 0:
                        for s in range(4):
                          for hx in range(2):
                            xs_ = xg.next()
                            xv = xs_.ap.rearrange("p a b -> p (a b)")
                            dma(xv, xs_.keys[0], xin[si].ap()[t0 + s * 128:t0 + (s + 1) * 128, 1024 * hx:1024 * (hx + 1)], ("xin", si))
                            for g2 in range(2):
                                g = 2 * hx + g2
                                pt = pS.next()
                                for q in range(4):
                                    kq = 4 * g2 + q
                                    tr(pt.ap[:, q * 128:(q + 1) * 128], pt.keys[0], xv[:, kq * 128:(kq + 1) * 128], xs_.keys[0], ident_f, CM)
                                cp(evac_eng(), big32[:, 4 * g:4 * g + 4, s * 128:(s + 1) * 128], ("big32", g),
                                   pt.ap.rearrange("p (a b) -> p a b", a=4), pt.keys[0])
                        for g in range(4):
                            dma(XT.ap()[4 * g:4 * g + 4, :, t0:t0 + TBK].rearrange("k p t -> p k t"), ("XT", b), big32[:, 4 * g:4 * g + 4, :],
                                ("big32", g), eng="pool")
                    else:
                        for g in range(4):
                            dma(big32[:, 4 * g:4 * g + 4, :], ("big32", g),
                                XT.ap()[4 * g:4 * g + 4, :, t0:t0 + TBK].rearrange("k p t -> p k t"), ("XT", b))
                    rms_stats(lambda kc: big32[:, kc, :], [("big32", g) for g in range(4)])
                    for kc in range(16):
                        eng = "dve" if kc % 2 == 0 else "pool"
                        stt(eng, h16[:, kc, :], ("h16", kc), big32[:, kc, :], ("big32", kc // 4), gpre(kc), rstd[:], "rstd",
                            ALU.mult, ALU.mult, extra_r=[PPK])
                    tb = []
                    for k in range(4):
                        tv = tabr.next()
                        dma(tv.ap[:], tv.keys[0], TAB[k].ap()[:, t0:t0 + TBK], ("tab", si))
                        tb.append(tv)
                    keep = {}
                    for oc, (typ, j, segs) in enumerate(WCH):
                        wt = wA.next()
                        dma(wt.ap[:], wt.keys[0], WIN.ap()[l, oc].rearrange("p (a b) -> p a b", a=16), ("WIN", l, oc))
                        z = pS.next()
                        for kc in range(16):
                            mm(z.ap[:], z.keys[0], wt.ap[:, kc, :], wt.keys[0], h16[:, kc, :], ("h16", kc), kc == 0, kc == 15)
                        zk = z.keys[0]
                        if typ == "ab":
                            o = b16r.next()
                            cp("act", o.ap[:], o.keys[0], z.ap[:], zk)
                            dma(ABd.ap()[j][:, t0:t0 + TBK], ("AB", b), o.ap[:], o.keys[0], eng="pool")
                        elif typ in ("ac", "dg"):
                            o = f32r.next()
                            if typ == "ac":
                                cp("act", o.ap[:], o.keys[0], z.ap[:], zk)
                            else:
                                act(o.ap[:], o.keys[0], z.ap[:], zk, AF.Sigmoid)
                            keep[typ] = o
                        elif typ in ("av", "da"):
                            c = keep["ac" if typ == "av" else "dg"]
                            o = b16r.next()
                            tt("dve", o.ap[:], o.keys[0], z.ap[:], zk, c.ap[:], c.keys[0], ALU.mult)
                            if typ == "av":
                                dma(ACV.ap()[j][:, 2 + t0:2 + t0 + TBK], ("ACV", b), o.ap[:], o.keys[0], eng="pool")
                            else:
                                dma(Ud.ap()[j][:, 15 + t0:15 + t0 + TBK], ("U", b), o.ap[:], o.keys[0], eng="pool")
                        elif typ in ("qb", "kb", "qc", "kc"):
                            qn = f32r.next()
                            if typ in ("qb", "kb"):
                                sq = sqr.next()
                                act(sq.ap[:], sq.keys[0], z.ap[:], zk, AF.Square)
                                mm(pX[:], "pX", blk_b, CMB, sq.ap[:], sq.keys[0], True, True)
                                if typ == "qb":
                                    act(sdt[:], "sdt", pX[:], "pX", AF.Sqrt, bias=EPS6x64, scale=64.0, extra_r=["epst"])
                                else:
                                    act(sdt[:], "sdt", pX[:], "pX", AF.Sqrt, bias=EPS6, extra_r=["epst"])
                                rs = f32r.next()
                                recip(rs.ap[:], rs.keys[0], sdt[:], "sdt")
                                stt("dve", qn.ap[:], qn.keys[0], z.ap[:], zk, gq if typ == "qb" else gk, rs.ap[:], rs.keys[0],
                                    ALU.mult, ALU.mult, extra_r=[PPK])
                                rot, cs, sn = rotb_f, tb[0], tb[1]
                            else:
                                if typ == "qc":
                                    act(qn.ap[:], qn.keys[0], z.ap[:], zk, AF.Copy, scale=0.125)
                                else:
                                    cp("act", qn.ap[:], qn.keys[0], z.ap[:], zk)
                                rot, cs, sn = rotc_f, tb[2], tb[3]
                            mm(pX[:], "pX", rot, CM, qn.ap[:], qn.keys[0], True, True)
                            t1 = f32r.next()
                            tt("pool", t1.ap[:], t1.keys[0], qn.ap[:], qn.keys[0], cs.ap[:], cs.keys[0], ALU.mult)
                            t2 = f32r.next()
                            tt("dve", t2.ap[:], t2.keys[0], pX[:], "pX", sn.ap[:], sn.keys[0], ALU.mult)
                            o = b16r.next()
                            tt("dve", o.ap[:], o.keys[0], t1.ap[:], t1.keys[0], t2.ap[:], t2.keys[0], ALU.add)
                            if typ == "qb":
                                dma(QB.ap()[j][:, t0:t0 + TBK], ("QB", b), o.ap[:], o.keys[0], eng="pool")
                            elif typ == "kb":
                                dma(KBT.ap()[:, t0:t0 + TBK], ("KBT", b), o.ap[:], o.keys[0], eng="pool")
                            elif typ == "qc":
                                dma(QC.ap()[j][:, t0:t0 + TBK], ("QC", b), o.ap[:], o.keys[0], eng="pool")
                            else:
                                dma(KCT.ap()[:, 128 + t0:128 + t0 + TBK], ("KCT", b), o.ap[:], o.keys[0], eng="pool")
                        elif typ in ("vb", "vc"):
                            o = b16r.next()
                            cp("act", o.ap[:], o.keys[0], z.ap[:], zk)
                            pt = pT.next()
                            for s in range(4):
                                tr(pt.ap[:, s * 128:(s + 1) * 128], pt.keys[0], o.ap[:, s * 128:(s + 1) * 128], o.keys[0], ident_b, CMB)
                            vt = vt16.next()
                            cp("dve", vt.ap[:], vt.keys[0], pt.ap[:, 0:512].rearrange("p (a b) -> p a b", a=4), pt.keys[0])
                            if typ == "vb":
                                dma(VBd.ap()[t0:t0 + TBK, :].rearrange("(s p) c -> p s c", p=128), ("VB", b), vt.ap[:], vt.keys[0], eng="pool")
                            else:
                                dma(VCd.ap()[128 + t0:128 + t0 + TBK, :].rearrange("(s p) c -> p s c", p=128), ("VC", b), vt.ap[:], vt.keys[0], eng="pool")
                P.barrier()

                for b in range(NB):
                    dma(KT[:, b * TBK:(b + 1) * TBK], ("KT", b), KBT.ap()[:, b * TBK:(b + 1) * TBK], ("KBT", b))
                for qq in range(2):
                    a_ = vaA.next()
                    memset("pool", a_.ap[:], a_.keys[0], 1.0)
                    b_ = vaB.next()
                    memset("pool", b_.ap[:], b_.keys[0], 1.0)
                for b in range(NB):
                    src = VBd.ap()[b * TBK:(b + 1) * TBK, :].rearrange("(s p) c -> p s c", p=128)
                    vt = vt16.next()
                    dma(vt.ap[:], vt.keys[0], src, ("VB", b))
                    a_ = vaA.next()
                    cp("pool", a_.ap[:, :, 0:64], a_.keys[0], vt.ap[:, :, 0:64], vt.keys[0])
                    dma(VAUG.ap()[0, 4 * b:4 * b + 4].rearrange("c k v -> k c v"), ("VAUG", b), a_.ap[:], a_.keys[0], eng="pool")
                    b_ = vaB.next()
                    cp("pool", b_.ap[:, :, 64:128], b_.keys[0], vt.ap[:, :, 64:128], vt.keys[0])
                    dma(VAUG.ap()[1, 4 * b:4 * b + 4].rearrange("c k v -> k c v"), ("VAUG", b), b_.ap[:], b_.keys[0], eng="pool")
                for qq in range(2):
                    qp = qpad.next()
                    memset("pool", qp.ap[:], qp.keys[0], 0.0)
                memset("pool", kcw[:], "kcw", 0.0)

                for b in range(NB):
                    t0 = b * TBK
                    for j in range(4):
                        a_ = abt.next()
                        dma(a_.ap[:], a_.keys[0], ABd.ap()[j][:, t0:t0 + TBK], ("AB", b))
                        v_ = acvt.next()
                        dma(v_.ap[:], v_.keys[0], ACV.ap()[j][:, t0 + 1:t0 + TBK + 3], [("ACV", b - 1), ("ACV", b), ("ACV", b + 1)])
                        acc = f32r.next()
                        ts("pool", acc.ap[:], acc.keys[0], v_.ap[:, 0:512], v_.keys[0], caw(j, 0), None, ALU.mult, extra_r=[PPK])
                        stt("pool", acc.ap[:], acc.keys[0], v_.ap[:, 1:513], v_.keys[0], caw(j, 1), acc.ap[:], acc.keys[0], ALU.mult, ALU.add, extra_r=[PPK])
                        stt("pool", acc.ap[:], acc.keys[0], v_.ap[:, 2:514], v_.keys[0], caw(j, 2), acc.ap[:], acc.keys[0], ALU.mult, ALU.add, extra_r=[PPK])
                        tt("pool", h16[:, j, :], ("h16", j), acc.ap[:], acc.keys[0], a_.ap[:], a_.keys[0], ALU.mult)
                    u_ = ut.next()
                    for j in range(4):
                        dma(u_.ap[:, j, :], u_.keys[0], Ud.ap()[j][:, t0:t0 + TBK + 30], [("U", b - 1), ("U", b), ("U", b + 1)])
                    for k in range(31):
                        for j in range(4):
                            eng = "dve" if j < 2 else "pool"
                            if k == 0:
                                ts(eng, cacc[:, j, :], ("cacc", j), u_.ap[:, j, 0:512], u_.keys[0], cdw(j, 0), cdb(j), ALU.mult, ALU.add, extra_r=[PPK])
                            else:
                                stt(eng, cacc[:, j, :], ("cacc", j), u_.ap[:, j, k:k + 512], u_.keys[0], cdw(j, k), cacc[:, j, :], ("cacc", j),
                                    ALU.mult, ALU.add, extra_r=[PPK])
                    for j in range(4):
                        mm(pX[:], "pX", ones512[:], "ones512", cacc[:, j, :], ("cacc", j), j == 0, j == 3)
                    for j in range(4):
                        tt("dve", cacc[:, j, :], ("cacc", j), cacc[:, j, :], ("cacc", j), pX[:], "pX", ALU.subtract)
                    sqs = []
                    for j in range(4):
                        sq = f32r.next()
                        act(sq.ap[:], sq.keys[0], cacc[:, j, :], ("cacc", j), AF.Square)
                        sqs.append(sq)
                    for j in range(4):
                        mm(pX[:], "pX", ones512[:], "ones512", sqs[j].ap[:], sqs[j].keys[0], j == 0, j == 3)
                    act(sdt[:], "sdt", pX[:], "pX", AF.Sqrt, bias=EPS5, extra_r=["epst"])
                    recip(rstd[:], "rstd", sdt[:], "sdt")
                    for j in range(4):
                        n_ = f32r.next()
                        tt("dve", n_.ap[:], n_.keys[0], cacc[:, j, :], ("cacc", j), rstd[:], "rstd", ALU.mult)
                        act(h16[:, 12 + j, :], ("h16", 12 + j), n_.ap[:], n_.keys[0], AF.Silu, bias=lnb(j), scale=lng(j), extra_r=[PPK])
                    dma(kcw[0:64, 0, :], "kcw", KCT.ap()[0:64, t0:t0 + 768], [("KCT", b - 1), ("KCT", b), ("KCT", b + 1)])
                    dma(kcw[64:128, 1, :], "kcw", KCT.ap()[64:128, t0:t0 + 768], [("KCT", b - 1), ("KCT", b), ("KCT", b + 1)])
                    dma(vcw[:], "vcw", VCd.ap()[t0:t0 + 768, :].rearrange("(s p) c -> p s c", p=128), [("VC", b - 1), ("VC", b), ("VC", b + 1)])
                    for j in range(4):
                        dma(qcblk[:, j, :], "qcblk", QC.ap()[j][:, t0:t0 + TBK], ("QC", b))
                    for s in range(4):
                        n = 4 * b + s
                        mi = 0 if n == 0 else (2 if n == NC_ - 1 else 1)
                        Rv = Rt.next()
                        for j in range(4):
                            for hf in range(2):
                                h = j + 4 * hf
                                sp_ = pS.next()
                                mm(sp_.ap[:, 0:384], sp_.keys[0], qcblk[:, j, s * 128:(s + 1) * 128], "qcblk",
                                   kcw[:, hf, s * 128:s * 128 + 384], "kcw", True, True)
                                sm = smr.next()
                                tt("dve", sm.ap[:], sm.keys[0], sp_.ap[:, 0:384], sp_.keys[0], mask_s[:, mi, :], "mask_s", ALU.add)
                                sv = smallr.next()
                                P.op("dve", lambda e, o=sv.ap[:, 0:1], i=sm.ap[:]: e.tensor_reduce(out=o, in_=i, axis=AX.X, op=ALU.max),
                                     reads=[sm], writes=[sv])
                                ts("dve", sv.ap[:, 1:2], sv.keys[0], sv.ap[:, 0:1], sv.keys[0], sink(h), -1.0, ALU.max, ALU.mult, extra_r=[PPK])
                                e_ = er.next()
                                act(e_.ap[:], e_.keys[0], sm.ap[:], sm.keys[0], AF.Exp, bias=sv.ap[:, 1:2], accum=sv.ap[:, 2:3],
                                    extra_r=[sv], extra_w=[sv])
                                act(sv.ap[:, 3:4], sv.keys[0], sv.ap[:, 1:2], sv.keys[0], AF.Exp, bias=sink(h), extra_r=[PPK])
                                tt("dve", sv.ap[:, 0:1], sv.keys[0], sv.ap[:, 2:3], sv.keys[0], sv.ap[:, 3:4], sv.keys[0], ALU.add)
                                recip(Rv.ap[:, h:h + 1], Rv.keys[0], sv.ap[:, 0:1], sv.keys[0])
                                pt = pT.next()
                                for c in range(3):
                                    tr(pt.ap[:, c * 128:(c + 1) * 128], pt.keys[0], e_.ap[:, c * 128:(c + 1) * 128], e_.keys[0], ident_b, CMB)
                                et = etr.next()
                                cp("pool" if False else evac_eng(), et.ap[:], et.keys[0], pt.ap[:, 0:384].rearrange("p (a b) -> p a b", a=3), pt.keys[0])
                                for c in range(3):
                                    mm(pO[0][:, h * 64:(h + 1) * 64], "pO0", et.ap[:, c, :], et.keys[0],
                                       vcw[:, s + c, hf * 64:(hf + 1) * 64], "vcw", c == 0, c == 2)
                        ob = osb.next()
                        for h in range(8):
                            if h % 2 == 0:
                                act(ob.ap[:, h * 64:(h + 1) * 64], ob.keys[0], pO[0][:, h * 64:(h + 1) * 64], "pO0", AF.Copy,
                                    scale=Rv.ap[:, h:h + 1], extra_r=[Rv])
                            else:
                                ts("dve", ob.ap[:, h * 64:(h + 1) * 64], ob.keys[0], pO[0][:, h * 64:(h + 1) * 64], "pO0",
                                   Rv.ap[:, h:h + 1], None, ALU.mult, extra_r=[Rv])
                        pt = pT.next()
                        for jj in range(4):
                            tr(pt.ap[:, jj * 128:(jj + 1) * 128], pt.keys[0], ob.ap[:, jj * 128:(jj + 1) * 128], ob.keys[0], ident_b, CMB)
                        cp(evac_eng(), h16[:, 8:12, s * 128:(s + 1) * 128], [("h16", 8), ("h16", 9), ("h16", 10), ("h16", 11)],
                           pt.ap[:, 0:512].rearrange("p (a b) -> p a b", a=4), pt.keys[0])
                    steps = [(j, hf, kc) for j in range(4) for hf in range(2) for kc in range(NC_)]
                    qps = {}
                    pts = {}

                    def rec_S(t):
                        j, hf, kc = steps[t]
                        if hf == 0 and kc == 0:
                            qp = qpad.next()
                            dma(qp.ap[0:64, 0, :], qp.keys[0], QB.ap()[j][0:64, t0:t0 + TBK], ("QB", b))
                            dma(qp.ap[64:128, 1, :], qp.keys[0], QB.ap()[j][64:128, t0:t0 + TBK], ("QB", b))
                            qps[j] = qp
                        qp = qps[j]
                        sp_ = pS.next()
                        mm(sp_.ap[:], sp_.keys[0], KT[:, kc * 128:(kc + 1) * 128], ("KT", kc // 4), qp.ap[:, hf, :], qp.keys[0], True, True)
                        pt_ = b16r.next()
                        act(pt_.ap[:], pt_.keys[0], sp_.ap[:], sp_.keys[0], AF.Exp)
                        pts[t] = pt_

                    rec_S(0)
                    rec_S(1)
                    vtile = None
                    for t in range(len(steps)):
                      j, hf, kc = steps[t]
                      if t + 2 < len(steps):
                          rec_S(t + 2)
                      if kc % VG == 0:
                          vtile = vring.next()
                          dma(vtile.ap[:, 0:VG, :], vtile.keys[0], VAUG.ap()[hf, kc:kc + VG].rearrange("c k v -> k c v"),
                              [("VAUG", bb) for bb in range(kc // 4, (kc + VG + 3) // 4)])
                      pt_ = pts.pop(t)
                      mm(pO[hf][:], "pO%d" % hf, vtile.ap[:, kc % VG, :], vtile.keys[0],
                         pt_.ap[:], pt_.keys[0], kc == 0, kc == NC_ - 1)
                      if hf == 1 and kc == NC_ - 1:
                        xr = f32r.next()
                        recip(xr.ap[64:128, :], xr.keys[0], pO[0][64:128, :], "pO0")
                        recip(xr.ap[0:64, :], xr.keys[0], pO[1][0:64, :], "pO1")
                        mm(pX[:], "pX", swap_f, CM, xr.ap[:], xr.keys[0], True, True)
                        sw = f32r.next()
                        cp("act", sw.ap[:], sw.keys[0], pX[:], "pX")
                        tt("dve", h16[0:64, 4 + j, :], ("h16", 4 + j), pO[0][0:64, :], "pO0", sw.ap[0:64, :], sw.keys[0], ALU.mult)
                        tt("dve", h16[64:128, 4 + j, :], ("h16", 4 + j), pO[1][64:128, :], "pO1", sw.ap[64:128, :], sw.keys[0], ALU.mult)
                    for oc in range(16):
                        wt = wA.next()
                        dma(wt.ap[:], wt.keys[0], WOUT.ap()[l, oc].rearrange("p (a b) -> p a b", a=16), ("WOUT", l, oc))
                        z = pS.next()
                        for kc in range(16):
                            mm(z.ap[:], z.keys[0], wt.ap[:, kc, :], wt.keys[0], h16[:, kc, :], ("h16", kc), kc == 0, kc == 15)
                        cp(evac_eng(), big32[:, oc, :], ("big32", oc // 4), z.ap[:], z.keys[0])
                    rms_stats(lambda kc: big32[:, kc, :], [("big32", g) for g in range(4)])
                    for g8 in range(8):
                        g = g8 // 2
                        xr_ = xg.next()
                        dma(xr_.ap[:], xr_.keys[0], XT.ap()[2 * g8:2 * g8 + 2, :, t0:t0 + TBK].rearrange("k p t -> p k t"), ("XT", b))
                        for q in range(2):
                            kc = 2 * g8 + q
                            stt("dve", big32[:, kc, :], ("big32", g), big32[:, kc, :], ("big32", g), gpost(kc),
                                rstd[:], "rstd", ALU.mult, ALU.mult, extra_r=[PPK])
                        tt("dve", xr_.ap[:], xr_.keys[0], xr_.ap[:], xr_.keys[0], big32[:, 2 * g8:2 * g8 + 2, :], ("big32", g), ALU.add)
                        dma(X1T.ap()[2 * g8:2 * g8 + 2, :, t0:t0 + TBK].rearrange("k p t -> p k t"), ("X1T", b), xr_.ap[:], xr_.keys[0], eng="pool")
                P.barrier()

                for b in range(NB):
                    t0 = b * TBK
                    for g in range(4):
                        dma(big32[:, 4 * g:4 * g + 4, :], ("big32", g),
                            X1T.ap()[4 * g:4 * g + 4, :, t0:t0 + TBK].rearrange("k p t -> p k t"), ("X1T", b))
                    rms_stats(lambda kc: big32[:, kc, :], [("big32", g) for g in range(4)])
                    for kc in range(16):
                        eng = "dve" if kc % 2 == 0 else "pool"
                        stt(eng, h16[:, kc, :], ("h16", kc), big32[:, kc, :], ("big32", kc // 4), gffn(kc), rstd[:], "rstd",
                            ALU.mult, ALU.mult, extra_r=[PPK])
                    for fc in range(64):
                        wt = wA.next()
                        dma(wt.ap[:], wt.keys[0], W1.ap()[l, fc].rearrange("p (a b) -> p a b", a=16), ("W1", l, fc))
                        z = pS.next()
                        for kc in range(16):
                            mm(z.ap[:], z.keys[0], wt.ap[:, kc, :], wt.keys[0], h16[:, kc, :], ("h16", kc), kc == 0, kc == 15)
                        r_ = f32r.next()
                        act(r_.ap[:], r_.keys[0], z.ap[:], z.keys[0], AF.Relu)
                        tt("dve" if fc % 2 == 0 else "pool", uT[:, fc, :], ("uT", fc), r_.ap[:], r_.keys[0], r_.ap[:], r_.keys[0], ALU.mult)
                    for oc in range(16):
                        z = pS.next()
                        for hh in range(4):
                            wt = wA.next()
                            dma(wt.ap[:], wt.keys[0], W2.ap()[l, oc].rearrange("p (a b) -> p a b", a=64)[:, 16 * hh:16 * hh + 16, :], ("W2", l, oc))
                            for q in range(16):
                                fc = 16 * hh + q
                                mm(z.ap[:], z.keys[0], wt.ap[:, q, :], wt.keys[0], uT[:, fc, :], ("uT", fc), fc == 0, fc == 63)
                        cp(evac_eng(), big32[:, oc, :], ("big32", oc // 4), z.ap[:], z.keys[0])
                    rms_stats(lambda kc: big32[:, kc, :], [("big32", g) for g in range(4)])
                    for g8 in range(8):
                        g = g8 // 2
                        xr_ = xg.next()
                        dma(xr_.ap[:], xr_.keys[0], X1T.ap()[2 * g8:2 * g8 + 2, :, t0:t0 + TBK].rearrange("k p t -> p k t"), ("X1T", b))
                        for q in range(2):
                            kc = 2 * g8 + q
                            stt("dve", big32[:, kc, :], ("big32", g), big32[:, kc, :], ("big32", g), gpffn(kc),
                                rstd[:], "rstd", ALU.mult, ALU.mult, extra_r=[PPK])
                        if l < NL - 1:
                            tt("dve", xr_.ap[:], xr_.keys[0], xr_.ap[:], xr_.keys[0], big32[:, 2 * g8:2 * g8 + 2, :], ("big32", g), ALU.add)
                            dma(XT.ap()[2 * g8:2 * g8 + 2, :, t0:t0 + TBK].rearrange("k p t -> p k t"), ("XT", b), xr_.ap[:], xr_.keys[0], eng="pool")
                        else:
                            tt("dve", big32[:, 2 * g8:2 * g8 + 2, :], ("big32", g), xr_.ap[:], xr_.keys[0], big32[:, 2 * g8:2 * g8 + 2, :], ("big32", g), ALU.add)
                    if l == NL - 1:
                        for s in range(4):
                          for hx in range(2):
                            ot_ = xg.next()
                            ot = TV(ot_.ap.rearrange("p a b -> p (a b)"), ot_.keys)
                            for g2 in range(2):
                                g = 2 * hx + g2
                                pt = pS.next()
                                for q in range(4):
                                    kc = 4 * g + q
                                    tr(pt.ap[:, q * 128:(q + 1) * 128], pt.keys[0], big32[:, kc, s * 128:(s + 1) * 128], ("big32", g), ident_f, CM)
                                cp(evac_eng(), ot.ap[:, 512 * g2:512 * (g2 + 1)], ot.keys[0], pt.ap[:], pt.keys[0])
                            dma(yout[si].ap()[t0 + s * 128:t0 + (s + 1) * 128, 1024 * hx:1024 * (hx + 1)], ("yout", si), ot.ap[:], ot.keys[0], eng="pool")
                P.barrier()

        P.op("sp", None, reads=[("yout", si) for si in range(len(SEGS))], kind="w")
        P.emit()
    return nc


_CACHE = {}


def run_segments(inp, xsegs, SEGS):
    key = tuple(SEGS)
    if key not in _CACHE:
        _CACHE[key] = build_program(SEGS)
    nc = _CACHE[key]
    cm, mask = const_mats()
    pp = pack_params(inp)
    tabs = [rope_tables(S) for S in SEGS]
    base = {"w_in": np.ascontiguousarray(inp["w_in"], np.float32), "w_out": np.ascontiguousarray(inp["w_out"], np.float32),
            "w_ff1": np.ascontiguousarray(inp["w_ff1"], np.float32), "w_ff2": np.ascontiguousarray(inp["w_ff2"], np.float32),
            "cm": cm, "mask": mask, "pp": pp}
    for si in range(len(SEGS)):
        for k in range(4):
            base["tab%d_%d" % (si, k)] = tabs[si][k]
    in_maps = []
    ncores = len(xsegs)
    for c in range(ncores):
        m = dict(base)
        for si in range(len(SEGS)):
            m["x%d" % si] = xsegs[c][si]
        in_maps.append(m)
    res = run_bass_kernel_spmd(nc, in_maps, core_ids=list(range(ncores)))
    return [[res.results[c]["y%d" % si] for si in range(len(SEGS))] for c in range(ncores)]


def kernel(**inputs):
    inp = {k: np.asarray(v) for k, v in inputs.items()}
    xp = np.asarray(inp["x_prompt"], np.float32)
    xs = np.asarray(inp["x_sample"], np.float32)
    SP, SS = xp.shape[1], xs.shape[1]
    xsegs = []
    for c in range(2):
        xsegs.append([np.ascontiguousarray(xp[c]), np.ascontiguousarray(xs[2 * c]), np.ascontiguousarray(xs[2 * c + 1])])
    outs = run_segments(inp, xsegs, [SP, SS, SS])
    yp = np.stack([outs[c][0] for c in range(2)], 0).astype(np.float32)
    ys = np.stack([outs[c // 2][1 + c % 2] for c in range(4)], 0).astype(np.float32)
    return (yp, ys)
```
